# Optimizing a Trainium2 kernel written in Bass

```python
import math
import jax, jax.numpy as jnp
from jax import lax
import numpy as np

D_MODEL = 1024
BATCH = 8
SEQ = 2048
DEPTH = 1

N_HEADS = 8
Q_LORA = 256
KV_LORA = 128
QK_NOPE = 64
QK_ROPE = 32
V_HEAD = 64
QK_HEAD = QK_NOPE + QK_ROPE
ATTN_SCALE = 1.0 / math.sqrt(QK_HEAD)
ROPE_THETA = 10000.0
Q_BLOCK = 128
CONV_CH = 512
CONV_K = 31
D_FF = ((8 * D_MODEL // 3 + 255) // 256) * 256
N_BRANCH = 2
D_IN = Q_LORA + KV_LORA + QK_ROPE + 2 * CONV_CH + N_BRANCH * D_MODEL
IN_SPLITS = (Q_LORA, Q_LORA + KV_LORA, Q_LORA + KV_LORA + QK_ROPE,
             Q_LORA + KV_LORA + QK_ROPE + 2 * CONV_CH)
N_MOD = 6
EPS_RMS = 1e-6
EPS_LN = 1e-5

kernel_name = "hybrid_mla_conformer_gated_encoder_block"


def rms_norm(x, g):
    xf = x.astype(jnp.float32)
    y = xf * lax.rsqrt(jnp.mean(xf * xf, axis=-1, keepdims=True) + EPS_RMS)
    return (y * g.astype(jnp.float32)).astype(x.dtype)


def layer_norm(x, g, b):
    xf = x.astype(jnp.float32)
    mu = jnp.mean(xf, axis=-1, keepdims=True)
    var = jnp.mean(jnp.square(xf - mu), axis=-1, keepdims=True)
    y = (xf - mu) * lax.rsqrt(var + EPS_LN)
    return (y * g.astype(jnp.float32) + b.astype(jnp.float32)).astype(x.dtype)


def rope_cos_sin(positions):
    inv_freq = ROPE_THETA ** (-jnp.arange(0, QK_ROPE, 2, dtype=jnp.float32) / QK_ROPE)
    ang = positions.astype(jnp.float32)[..., None] * inv_freq
    return jnp.cos(ang), jnp.sin(ang)


def apply_rope(x, cos, sin):
    shape = cos.shape[:2] + (1,) * (x.ndim - 3) + cos.shape[-1:]
    cos = cos.reshape(shape)
    sin = sin.reshape(shape)
    xf = x.astype(jnp.float32)
    x1, x2 = jnp.split(xf, 2, axis=-1)
    out = jnp.concatenate([x1 * cos - x2 * sin, x2 * cos + x1 * sin], axis=-1)
    return out.astype(x.dtype)


def mla_attention(q_nope, q_rope, k_nope, k_rope, v):
    B, S, H, _ = q_nope.shape
    nb = S // Q_BLOCK
    qn = q_nope.reshape(B, nb, Q_BLOCK, H, QK_NOPE).transpose(1, 0, 2, 3, 4)
    qr = q_rope.reshape(B, nb, Q_BLOCK, H, QK_ROPE).transpose(1, 0, 2, 3, 4)

    def block(args):
        qn_b, qr_b = args
        s = (jnp.einsum('bqhd,bkhd->bhqk', qn_b, k_nope)
             + jnp.einsum('bqhr,bkr->bhqk', qr_b, k_rope))
        p = jax.nn.softmax(s.astype(jnp.float32) * ATTN_SCALE, axis=-1).astype(v.dtype)
        return jnp.einsum('bhqk,bkhd->bqhd', p, v)

    o = lax.map(block, (qn, qr))
    return o.transpose(1, 0, 2, 3, 4).reshape(B, S, H * V_HEAD)


def depthwise_conv(z, w, b):
    out = lax.conv_general_dilated(
        z, w[:, None, :].astype(z.dtype), window_strides=(1,),
        padding=[(CONV_K // 2, CONV_K // 2)],
        dimension_numbers=('NWC', 'WIO', 'NWC'),
        feature_group_count=z.shape[-1])
    return out + b.astype(z.dtype)


def setup_inputs(seed: int = 0) -> dict:
    key = jax.random.key(seed)
    ks = jax.random.split(key, 24)
    f32 = jnp.float32

    def w(k, shape, fan_in, mult=1.0):
        return jax.random.normal(k, shape, f32) * (mult * fan_in ** -0.5)

    def gain(k, shape):
        return 1.0 + 0.02 * jax.random.normal(k, shape, f32)

    x = jax.random.normal(ks[0], (BATCH, SEQ, D_MODEL), f32)
    c = jax.random.normal(ks[1], (BATCH, D_MODEL), f32)
    offsets = jax.random.randint(ks[2], (BATCH, 1), 0, 4096, dtype=jnp.int32)
    positions = (jnp.arange(SEQ, dtype=jnp.int32)[None, :] + offsets).astype(jnp.int32)
    L = DEPTH
    return {
        "x": x,
        "c": c,
        "positions": positions,
        "w_ada": w(ks[3], (L, D_MODEL, N_MOD * D_MODEL), D_MODEL, 0.5),
        "b_ada": 0.02 * jax.random.normal(ks[4], (L, N_MOD * D_MODEL), f32),
        "g_norm_mix": gain(ks[5], (L, D_MODEL)),
        "w_in": w(ks[6], (L, D_MODEL, D_IN), D_MODEL),
        "g_q_a": gain(ks[7], (L, Q_LORA)),
        "w_q_up": w(ks[8], (L, Q_LORA, N_HEADS * QK_HEAD), Q_LORA),
        "g_kv_a": gain(ks[9], (L, KV_LORA)),
        "w_kv_up": w(ks[10], (L, KV_LORA, N_HEADS * (QK_NOPE + V_HEAD)), KV_LORA),
        "w_attn_o": w(ks[11], (L, N_HEADS * V_HEAD, D_MODEL), N_HEADS * V_HEAD),
        "w_dw": w(ks[12], (L, CONV_K, CONV_CH), CONV_K),
        "b_dw": 0.02 * jax.random.normal(ks[13], (L, CONV_CH), f32),
        "g_conv_ln": gain(ks[14], (L, CONV_CH)),
        "b_conv_ln": 0.02 * jax.random.normal(ks[15], (L, CONV_CH), f32),
        "w_conv_out": w(ks[16], (L, CONV_CH, D_MODEL), CONV_CH),
        "w_out": w(ks[17], (L, D_MODEL, D_MODEL), D_MODEL),
        "g_norm_ffn": gain(ks[18], (L, D_MODEL)),
        "w_ffn_gate": w(ks[19], (L, D_MODEL, D_FF), D_MODEL),
        "w_ffn_up": w(ks[20], (L, D_MODEL, D_FF), D_MODEL),
        "w_ffn_down": w(ks[21], (L, D_FF, D_MODEL), D_FF),
        "g_final": gain(ks[22], (D_MODEL,)),
    }


def reference(x, c, positions, w_ada, b_ada, g_norm_mix, w_in, g_q_a, w_q_up, g_kv_a, w_kv_up,
              w_attn_o, w_dw, b_dw, g_conv_ln, b_conv_ln, w_conv_out, w_out, g_norm_ffn,
              w_ffn_gate, w_ffn_up, w_ffn_down, g_final):
    B, S, D = x.shape
    cos, sin = rope_cos_sin(positions)
    c_act = jax.nn.silu(c)

    for l in range(DEPTH):
        mod = c_act @ w_ada[l] + b_ada[l]
        shift_m, scale_m, gate_m, shift_f, scale_f, gate_f = [
            m[:, None, :] for m in jnp.split(mod, N_MOD, axis=-1)]

        h = rms_norm(x, g_norm_mix[l]) * (1.0 + scale_m) + shift_m
        proj = h @ w_in[l]
        q_a, kv_a, k_rope, conv_in, gate_logits = jnp.split(proj, IN_SPLITS, axis=-1)

        q = (rms_norm(q_a, g_q_a[l]) @ w_q_up[l]).reshape(B, S, N_HEADS, QK_HEAD)
        q_nope, q_rope = q[..., :QK_NOPE], apply_rope(q[..., QK_NOPE:], cos, sin)
        kv = (rms_norm(kv_a, g_kv_a[l]) @ w_kv_up[l]).reshape(B, S, N_HEADS, QK_NOPE + V_HEAD)
        k_nope, v = kv[..., :QK_NOPE], kv[..., QK_NOPE:]
        k_rope = apply_rope(k_rope, cos, sin)
        y_a = mla_attention(q_nope, q_rope, k_nope, k_rope, v) @ w_attn_o[l]

        u, u_gate = jnp.split(conv_in, 2, axis=-1)
        z = u * jax.nn.sigmoid(u_gate)
        z = depthwise_conv(z, w_dw[l], b_dw[l])
        z = jax.nn.silu(layer_norm(z, g_conv_ln[l], b_conv_ln[l]))
        y_b = z @ w_conv_out[l]

        g_a, g_b = jnp.split(gate_logits, N_BRANCH, axis=-1)
        merged = jax.nn.sigmoid(g_a) * y_a + jax.nn.sigmoid(g_b) * y_b
        x = x + gate_m * (merged @ w_out[l])

        h2 = rms_norm(x, g_norm_ffn[l]) * (1.0 + scale_f) + shift_f
        ff = (jax.nn.silu(h2 @ w_ffn_gate[l]) * (h2 @ w_ffn_up[l])) @ w_ffn_down[l]
        x = x + gate_f * ff

    return rms_norm(x, g_final)
```

```python
import math
import numpy as np
import concourse.bass as bass
import concourse.mybir as mybir
from concourse.bass_utils import run_bass_kernel_spmd

F32 = mybir.dt.float32
BF16 = mybir.dt.bfloat16
I32 = mybir.dt.int32
ALU = mybir.AluOpType
AF = mybir.ActivationFunctionType

D = 1024
S = 2048
NCORES = 8
NH = 8
DFF = 2816
NFF = DFF // 128
DIN = 3488
GT = 512
NG = S // GT
EPS_RMS = 1e-6
EPS_LN = 1e-5
ATTN_SCALE = 1.0 / math.sqrt(96.0)
C_QA, C_KVA, C_KR, C_U, C_UG, C_GA, C_GB = 0, 256, 384, 416, 928, 1440, 2464
V_GMIX, V_GFFN, V_GQA, V_GKV, V_BDW, V_GLN, V_BLN, V_WDW, V_INVF, V_C = 0, 8, 16, 18, 19, 23, 27, 31, 155, 156
V_BADA, V_GFIN = 164, 212
NVEC = 220
TWO_PI = 2.0 * math.pi
RC1 = 6.28125
RC2 = TWO_PI - RC1
PI_CL = 3.1415925


class StopBuild(Exception):
    pass


class Buf:
    __slots__ = ("w", "r", "name")

    def __init__(self, name=""):
        self.w = None
        self.r = {}
        self.name = name


class Sched:
    ENGS = ("pe", "act", "dve", "pool", "sp")

    def __init__(self, nc, ndma=32):
        self.nc = nc
        self.q = {e: [] for e in self.ENGS}
        self.cnt = {e: 0 for e in self.ENGS}
        self.seen = {e: {} for e in self.ENGS}
        self.sems = {}
        self.ndma = ndma
        self.dcnt = [0] * ndma
        self.dnext = 0
        self.dnext_q = {"sp": 0, "pool": 0}
        self.ninst = {e: 0 for e in self.ENGS}

    def alloc_sems(self, stack):
        for e in ("pe", "act", "dve", "pool"):
            self.sems[e] = stack.enter_context(self.nc.semaphore("s_" + e))
        for i in range(self.ndma):
            self.sems[("d", i)] = stack.enter_context(self.nc.semaphore("s_d%d" % i))

    def _waits(self, eng, deps):
        best = {}
        for d in deps:
            if d is None:
                continue
            k, v = d
            if k == eng == "pe":
                continue
            if v > best.get(k, 0):
                best[k] = v
        for k, v in best.items():
            if self.seen[eng].get(k, 0) >= v:
                continue
            self.seen[eng][k] = v
            self.q[eng].append(lambda e, k=k, v=v: e.wait_ge(self.sems[k], v))

    def op(self, eng, fn, reads=(), writes=(), ms=True):
        deps = []
        for b in reads:
            deps.append(b.w)
        for b in writes:
            deps.append(b.w)
            deps.extend(b.r.values())
        self._waits(eng, deps)
        if ms:
            self.cnt[eng] += 1
            tk = (eng, self.cnt[eng])
            self.q[eng].append(lambda e, fn=fn, eng=eng: fn(e).then_inc(self.sems[eng], 1))
        else:
            tk = (eng, self.cnt[eng] + 1)
            self.q[eng].append(lambda e, fn=fn: fn(e))
        self.ninst[eng] += 1
        for b in reads:
            b.r[eng] = tk
        for b in writes:
            b.w = tk
            b.r = {}
        return tk

    def dma(self, eng, out, in_, reads=(), writes=(), after=()):
        lo, n = (0, 12) if eng == "sp" else (12, self.ndma - 12)
        i = lo + self.dnext_q[eng] % n
        self.dnext_q[eng] += 1
        key = ("d", i)
        deps = list(after)
        if self.dcnt[i] > 0:
            deps.append((key, self.dcnt[i]))
        for b in reads:
            deps.append(b.w)
        for b in writes:
            deps.append(b.w)
            deps.extend(b.r.values())
        self._waits(eng, deps)
        self.dcnt[i] += 16
        tk = (key, self.dcnt[i])
        self.q[eng].append(lambda e, out=out, in_=in_, key=key: e.dma_start(out=out, in_=in_).then_inc(self.sems[key], 16))
        self.ninst[eng] += 1
        for b in reads:
            b.r[key] = tk
        for b in writes:
            b.w = tk
            b.r = {}
        return tk

    def wait_all(self, eng, tickets):
        self._waits(eng, tickets)


def build_nc(dbg=None):
    import contextlib
    nc = bass.Bass("TRN2", target_bir_lowering=False)
    dt_in = {}

    def din(name, shape, dtype=F32):
        dt_in[name] = nc.dram_tensor(name, list(shape), dtype, kind="ExternalInput")
        return dt_in[name].ap()

    x_d = din("x", [S, D])
    pos_h = nc.dram_tensor("pos", [1, S], I32, kind="ExternalInput")
    wada_d = din("w_ada", [D, 6 * D])
    win_d = din("w_in_c", [27, 128, 8, 128])
    wkr_d = din("w_k_r", [D, 32])
    wq_d = din("w_q_up", [256, 768])
    wqrot_d = din("w_q_rot", [256, 256])
    wkrot_d = din("w_k_rot", [D, 32])
    wkv_d = din("w_kv_up", [128, 1024])
    wao_d = din("w_attn_o_c", [8, 64, NH, 128])
    wco_d = din("w_conv_out_c", [8, 128, 4, 128])
    wout_d = din("w_out", [8, 128, D])
    wg_d = din("w_ffn_gate_c", [NFF, 128, 8, 128])
    wu_d = din("w_ffn_up_c", [NFF, 128, 8, 128])
    wd_d = din("w_ffn_down", [NFF, 128, D])
    vecs_d = din("vecs", [128, NVEC])
    ident_d = din("ident", [128, 128])
    out_d = nc.dram_tensor("out", [S, D], F32, kind="ExternalOutput").ap()
    dbg_d = None
    if dbg:
        dbg_d = nc.dram_tensor("dbg", [128, dbg["cols"]], F32, kind="ExternalOutput").ap()

    def sb(name, shape, dtype):
        return nc.alloc_sbuf_tensor(name, list(shape), dtype)

    kT = sb("kT", [128, NH, S], BF16)
    vaug = sb("vaug", [128, 16, NH, 66], BF16)
    zz = sb("zz", [128, 4, S + 32], BF16)
    zT = zz
    zsT = zz
    cosT = sb("cosT", [128, S], BF16)
    sinT = sb("sinT", [128, S], BF16)
    vecs = sb("vecs_sb", [128, NVEC], F32)
    ident = sb("ident_sb", [128, 128], BF16)
    ones_bf = sb("ones_bf", [128, 4, 128], BF16)
    ones_f = sb("ones_f", [128, 128], F32)
    epsc = sb("epsc", [128, 2], F32)
    modT = sb("modT", [128, 48], F32)
    gsT = sb("gsT", [128, 16], F32)
    caT = sb("caT", [128, 8], BF16)
    gate_bc = sb("gate_bc", [128, 2 * D], F32)
    gfin_bc = sb("gfin_bc", [128, D], F32)
    wkv = sb("wkv", [128, 1024], BF16)
    wv = sb("wv", [128, 8, 64], BF16)
    wkrot = sb("wkrot", [128, 8, 32], BF16)
    wkr = sb("wkr", [128, 8, 32], BF16)
    U2 = sb("U2", [128, 10240], BF16)
    U2f = U2.bitcast(F32)
    zc = U2f[:, 0:2048].rearrange("p (c n) -> p c n", c=4)
    zcb = U2[:, 4096:6144].rearrange("p (c n) -> p c n", c=4)
    mean_sb = U2f[:, 3072:3584]
    m2 = U2f[:, 3584:4096]
    xgB = U2f[:, 0:4096].rearrange("p (j n) -> p j n", j=4)
    wq = U2[:, 8192:9728].rearrange("p (k n) -> p k n", k=2)
    wqrot = U2[:, 9728:10240].rearrange("p (k n) -> p k n", k=2)
    NW = 6
    wbuf = [sb("wbuf%d" % i, [128, 8, 128], BF16) for i in range(NW)]
    xg = sb("xg", [128, 4, D], F32)
    xn = sb("xn", [128, 4, D], BF16)
    ss = sb("ss", [128, 4], F32)
    sstmp = sb("sstmp", [128, 4], F32)
    rstd = sb("rstd", [128, 4], F32)
    hT = sb("hT", [128, 8, GT], BF16)
    sq = sb("sq", [128, 4, GT], BF16)
    junk = sq[:, :, :].rearrange("p a n -> p (a n)")[:, 0:D]
    nrm_tmp = sb("nrm_tmp", [128, GT], F32)
    nrm_rs = sb("nrm_rs", [128, GT], F32)
    kvnT = sb("kvnT", [128, GT], BF16)
    qanT = sb("qanT", [128, 2, GT], BF16)
    rp1 = nrm_tmp
    rp2 = nrm_rs
    sig = sb("sig", [128, 2, GT], F32)
    U1 = sb("U1", [128, 16896], BF16)
    U1f = U1.bitcast(F32)
    xgf = xg[:, :, :].rearrange("p j n -> p (j n)")
    rop = U2f[:, 0:4096].rearrange("p (i n) -> p i n", i=4)
    ropi = U2.bitcast(I32)[:, 0:4096].rearrange("p (i n) -> p i n", i=4)
    dgt = U1f[:, 7936:8448].rearrange("p (i n) -> p i n", i=4)
    dg = U1[:, 0:15872].rearrange("p (c k m) -> p c k m", c=4, k=31)
    NFH = NFF // 2
    actT = U1[:, 0:5632].rearrange("p (f n) -> p f n", f=NFH)
    mT = U1[:, 0:4096].rearrange("p (c n) -> p c n", c=8)
    qT = U1[:, 5632:9728].rearrange("p (h n) -> p h n", h=NH)
    NPT = 2
    pT = [U1[:, 9728:10752].rearrange("p (a n) -> p a n", a=2), U1[:, 10752:11776].rearrange("p (a n) -> p a n", a=2)]
    aoT = U1[0:64, 11776:15872].rearrange("p (h n) -> p h n", h=NH)
    evt = U1f[:, 7936:8448]
    rden = evt
    rb = nrm_tmp[0:64, :]
    dbgt = sb("dbgt", [128, GT], F32) if dbg else None

    PSB = [nc.alloc_psum_tensor("psb%d" % i, [128, 1024], F32) for i in range(4)]

    def bank(b):
        return PSB[b // 2][:, (b % 2) * 512:(b % 2) * 512 + 512]

    def bank_rows(b, r0, r1):
        return PSB[b // 2][r0:r1, (b % 2) * 512:(b % 2) * 512 + 512]

    PSBF = [p.bitcast(BF16) for p in PSB]

    sc = Sched(nc)
    B = {}

    def buf(name):
        if name not in B:
            B[name] = Buf(name)
        return B[name]

    PB = [buf("bank%d" % i) for i in range(8)]

    pe, act, dve, pool, sp = "pe", "act", "dve", "pool", "sp"

    class Scr:
        def __init__(self, name, src, shape, step):
            self.name, self.src, self.step = name, src, step
            self.scr = nc.dram_tensor("scr_" + name, list(shape), BF16).ap()
            self.n0 = shape[0]

        def cast(self, lo=0, hi=None, after=()):
            npieces = (self.n0 + self.step - 1) // self.step
            hi = npieces if hi is None else hi
            for pi_ in range(lo, hi):
                a = pi_ * self.step
                b_ = min(self.n0, a + self.step)
                sc.dma(pool, self.scr[a:b_], self.src[a:b_], writes=[buf("scr_%s_%d" % (self.name, pi_))], after=after)

        def get(self, idx):
            return self.scr[idx], buf("scr_%s_%d" % (self.name, idx // self.step))

    S_win = Scr("win", win_d, [27, 128, 8, 128], 3)
    S_wco = Scr("wco", wco_d, [8, 128, 4, 128], 4)
    S_wao = Scr("wao", wao_d, [8, 64, NH, 128], 4)
    S_wout = Scr("wout", wout_d, [8, 128, D], 2)
    S_wg = Scr("wg", wg_d, [NFF, 128, 8, 128], 2)
    S_wu = Scr("wu", wu_d, [NFF, 128, 8, 128], 2)
    S_wd = Scr("wd", wd_d, [NFF, 128, D], 2)
    WIN_CH = {c0: i_ for i_, c0 in enumerate(WIN_COLS)}

    def checkpoint(name):
        if dbg and dbg.get("stop") == name:
            raise StopBuild()

    out_tickets = []
    try:
        _emit_all = True
    except Exception:
        pass
    def emit_all():
        dbg_st = {"off": 0}

        def dump(name, ap_fn, bufs, rows=128, cols=GT):
            if not dbg or name not in dbg["want"]:
                return
            off = dbg_st["off"]
            dbg_st["off"] += cols
            dbg.setdefault("layout", {})[name] = (off, rows, cols)
            sc.op(dve, lambda e: e.tensor_copy(out=dbgt[0:rows, 0:cols], in_=ap_fn()), reads=bufs, writes=[buf("dbgt")])
            sc.dma(sp, dbg_d[0:rows, off:off + cols], dbgt[0:rows, 0:cols], reads=[buf("dbgt")], writes=[buf("dbg_out")])

        st = {"w": 0, "wd": 0, "ps": 0}
        def dump_setup():
            dump("modT", lambda: modT[:, :], [buf("modT")], cols=48)
            dump("gate", lambda: gate_bc[:, 0:512], [buf("gate_bc")])
            dump("gatef", lambda: gate_bc[:, D:D + 512], [buf("gate_bc")])
            dump("cos", lambda: cosT[64:96, 0:512], [buf("cs")], rows=32)
            dump("sin", lambda: sinT[64:96, 1536:2048], [buf("cs")], rows=32)

        def barrier():
            tks = [(e_, sc.cnt[e_]) for e_ in ("pe", "act", "dve", "pool") if sc.cnt[e_] > 0]
            for e_ in ("pe", "act", "dve", "pool", "sp"):
                sc.wait_all(e_, tks)

        sc.dma(sp, vecs[:, :], vecs_d, writes=[buf("vecs")])
        sc.dma(pool, ident[:, :], ident_d, writes=[buf("ident")])
        for i_ in range(2):
            sc.dma(pool, wkv[:, i_ * 512:(i_ + 1) * 512], wkv_d[:, i_ * 512:(i_ + 1) * 512], writes=[buf("wkv")])
        sc.dma(pool, wv[:, :, :], wkv_d.rearrange("p (h e) -> p h e", h=NH)[:, :, 64:128], writes=[buf("wv")])
        sc.dma(pool, wkrot[:, :, :], wkrot_d.rearrange("(k p) n -> p k n", p=128), writes=[buf("wkrot")])
        sc.dma(pool, wkr[:, :, :], wkr_d.rearrange("(k p) n -> p k n", p=128), writes=[buf("wkr")])
        sc.op(pool, lambda e: e.tensor_scalar(out=wkrot[:, :, 0:16], in0=wkrot[:, :, 0:16], scalar1=-1.0, scalar2=None, op0=ALU.mult),
              reads=[buf("wkrot")], writes=[buf("wkrot")])
        sc.op(dve, lambda e: e.memset(ones_f[:, :], 1.0), writes=[buf("ones_f")])
        sc.op(dve, lambda e: e.memset(epsc[:, 0:1], EPS_RMS), writes=[buf("epsc")])
        sc.op(dve, lambda e: e.memset(epsc[:, 1:2], EPS_LN), writes=[buf("epsc")])
        for i, val in enumerate((1.0 / 128, 1.0 / 256, 1.0 / 512, 1.0)):
            sc.op(dve, lambda e, i=i, val=val: e.memset(ones_bf[:, i, :], val), writes=[buf("ones_bf")])
        sc.op(act, lambda e: e.activation(out=caT[:, :], in_=vecs[:, V_C:V_C + 8], func=AF.Silu),
              reads=[buf("vecs")], writes=[buf("caT")])

        wada_v = wada_d.rearrange("(k p) n -> p k n", p=128)

        def mod_chunks(j_lo, j_hi, trig=(), evac=True):
            for j in range(j_lo, j_hi):
                i = st["w"] % NW
                st["w"] += 1
                w = wbuf[i]
                wb_ = buf("wbuf%d" % i)
                sc.dma(pool, w[:, :, :], wada_v[:, :, j * 128:(j + 1) * 128], writes=[wb_], after=trig)
                for k in range(8):
                    sc.op(pe, lambda e, w=w, k=k, j=j: e.matmul(PSB[2][:, j:j + 1], w[:, k, :], caT[:, k:k + 1], start=(k == 0), stop=(k == 7)),
                          reads=[buf("caT"), wb_], writes=[PB[4]], ms=(k == 7))
            if evac:
                mod_evac(j_lo, j_hi)

        def mod_evac(j_lo, j_hi):
            sc.op(dve, lambda e: e.tensor_tensor(out=modT[:, j_lo:j_hi], in0=PSB[2][:, j_lo:j_hi], in1=vecs[:, V_BADA + j_lo:V_BADA + j_hi],
                                                 op=ALU.add),
                  reads=[PB[4], buf("vecs")], writes=[buf("modT")])

        def mod_gs(half):
            j0 = half * 24
            gcol = V_GMIX if half == 0 else V_GFFN
            sc.op(dve, lambda e: e.scalar_tensor_tensor(out=gsT[:, half * 8:half * 8 + 8], in0=modT[:, j0 + 8:j0 + 16], scalar=1.0,
                                                        in1=vecs[:, gcol:gcol + 8], op0=ALU.add, op1=ALU.mult),
                  reads=[buf("modT"), buf("vecs")], writes=[buf("gsT")])

        def bcast_cols(src_fn, src_bufs, dst, doff):
            for c in range(8):
                pb = 5 + c // 4
                sc.op(dve, lambda e, c=c: e.tensor_scalar(out=dgt[:, c % 4, :], in0=ident[:, :], scalar1=src_fn(c), scalar2=None, op0=ALU.mult),
                      reads=src_bufs + [buf("ident")], writes=[buf("dgt%d" % (c % 4))])
                sc.op(pe, lambda e, c=c, pb=pb: e.matmul(bank(pb)[:, (c % 4) * 128:(c % 4 + 1) * 128], ones_f[:, :], dgt[:, c % 4, :], start=True, stop=True),
                      reads=[buf("ones_f"), buf("dgt%d" % (c % 4))], writes=[PB[pb]])
            for i in range(2):
                sc.op(dve, lambda e, i=i: e.tensor_copy(out=dst[:, doff + i * 512:doff + (i + 1) * 512], in_=bank(5 + i)),
                      reads=[PB[5 + i]], writes=[buf("gate_bc")])

        sc.op(dve, lambda e: e.memset(vaug[:, :, :, :], 0.0), writes=[buf("vaug")])
        sc.op(dve, lambda e: e.memset(vaug[:, :, :, 64:65], 1.0), writes=[buf("vaug")])
        sc.op(dve, lambda e: e.memset(zT[:, :, :], 0.0), writes=[buf("zT")])
        mod_chunks(0, 16, evac=False)
        S_win.cast(0, 3)
        R = slice(64, 96)
        rbuf = [buf("rop")]
        for cg in range(2):
            csl = slice(cg * 1024, (cg + 1) * 1024)
            sc.dma(sp, ropi[R, 0, :], bass.AP(pos_h, cg * 1024, [[0, 32], [1, 1024]]), writes=rbuf)
            sc.op(dve, lambda e: e.tensor_copy(out=rop[R, 1, :], in_=ropi[R, 0, :]), reads=rbuf, writes=rbuf)
            sc.op(dve, lambda e: e.tensor_scalar(out=rop[R, 1, :], in0=rop[R, 1, :], scalar1=vecs[R, V_INVF:V_INVF + 1], scalar2=None,
                                                 op0=ALU.mult), reads=rbuf + [buf("vecs")], writes=rbuf)
            for which, dst in ((0, sinT), (1, cosT)):
                si = 1
                if which == 1:
                    sc.op(dve, lambda e: e.tensor_scalar(out=rop[R, 0, :], in0=rop[R, 1, :], scalar1=math.pi / 2, scalar2=None,
                                                         op0=ALU.add), reads=rbuf, writes=rbuf)
                    si = 0
                sc.op(dve, lambda e, si=si: e.tensor_scalar(out=rop[R, 2, :], in0=rop[R, si, :], scalar1=1.0 / TWO_PI, scalar2=None,
                                                           op0=ALU.mult), reads=rbuf, writes=rbuf)
                sc.op(dve, lambda e: e.tensor_copy(out=ropi[R, 3, :], in_=rop[R, 2, :]), reads=rbuf, writes=rbuf)
                sc.op(dve, lambda e: e.tensor_copy(out=rop[R, 2, :], in_=ropi[R, 3, :]), reads=rbuf, writes=rbuf)
                sc.op(dve, lambda e, si=si: e.scalar_tensor_tensor(out=rop[R, 3, :], in0=rop[R, 2, :], scalar=-RC1, in1=rop[R, si, :],
                                                                  op0=ALU.mult, op1=ALU.add), reads=rbuf, writes=rbuf)
                sc.op(dve, lambda e: e.scalar_tensor_tensor(out=rop[R, 3, :], in0=rop[R, 2, :], scalar=-RC2, in1=rop[R, 3, :],
                                                            op0=ALU.mult, op1=ALU.add), reads=rbuf, writes=rbuf)
                sc.op(dve, lambda e: e.tensor_scalar(out=rop[R, 3, :], in0=rop[R, 3, :], scalar1=-PI_CL, scalar2=PI_CL,
                                                     op0=ALU.max, op1=ALU.min), reads=rbuf, writes=rbuf)
                sc.op(act, lambda e, dst=dst, csl=csl: e.activation(out=dst[R, csl], in_=rop[R, 3, :], func=AF.Sin),
                      reads=rbuf, writes=[buf("cs")])

        for c in range(4):
            id_b = bass.AP(ident, 0, [[128, 128], [0, 31], [1, 128]])
            w_b = bass.AP(vecs, V_WDW + c * 31, [[NVEC, 128], [1, 31], [0, 128]])
            sc.op(dve, lambda e, c=c, id_b=id_b, w_b=w_b: e.tensor_tensor(out=dg[:, c, :, :], in0=id_b, in1=w_b, op=ALU.mult),
                  reads=[buf("ident"), buf("vecs")], writes=[buf("dg%d" % c)])

        mod_evac(0, 16)
        mod_gs(0)

        def mod_late(part, trig=()):
            if part == 0:
                mod_chunks(16, 24, trig)
                bcast_cols(lambda c: modT[:, 16 + c:17 + c], [buf("modT")], gate_bc, 0)
            mod_chunks(24 + part * 6, 30 + part * 6, trig)
            if part == 3:
                mod_gs(1)
                bcast_cols(lambda c: modT[:, 40 + c:41 + c], [buf("modT")], gate_bc, D)
                bcast_cols(lambda c: vecs[:, V_GFIN + c:V_GFIN + c + 1], [buf("vecs")], gfin_bc, 0)
                dump_setup()

        checkpoint('setup')


        def load_w(W_view, col0, ncols, nk):
            i = st["w"] % NW
            st["w"] += 1
            scr_, idx_fn = W_view
            ap_, sbuf_ = scr_.get(idx_fn(col0))
            sc.dma(sp, wbuf[i][:, 0:nk, :], ap_, reads=[sbuf_], writes=[buf("wbuf%d" % i)])
            return wbuf[i], buf("wbuf%d" % i)

        def next_bank(lst):
            b = lst[st["ps"] % len(lst)]
            st["ps"] += 1
            return b

        win_v = (S_win, lambda col0: WIN_CH[col0])
        wg_v = (S_wg, lambda col0: col0 // 128)
        wu_v = (S_wu, lambda col0: col0 // 128)
        wco_v = (S_wco, lambda col0: col0 // 128)

        def lin_T(pb, W_view, col0, ncols, nk, rhs_fn, rhs_bufs, out_rows=None, tile_pos=None, wt=None):
            if wt is None:
                w, wb_ = load_w(W_view, col0, ncols, nk)
                lhs = lambda k: w[:, k, 0:ncols]
            else:
                lhs, wb_ = wt
            o = bank(pb) if out_rows is None else bank_rows(pb, out_rows[0], out_rows[1])
            if out_rows is None and ncols < 128:
                o = bank_rows(pb, 0, ncols)
            for k in range(nk):
                kw = {}
                if tile_pos is not None:
                    kw["tile_position"] = tile_pos
                sc.op(pe, lambda e, o=o, k=k, lhs=lhs, kw=kw: e.matmul(o, lhs(k), rhs_fn(k), start=(k == 0), stop=(k == nk - 1), **kw),
                      reads=[wb_] + rhs_bufs, writes=[PB[pb]], ms=(k == nk - 1))

        def load_x(g, xb=None, pre="xg"):
            xb = xg if xb is None else xb
            for j in range(4):
                t = g * 4 + j
                sc.dma(pool, xb[:, j, :], x_d[t * 128:(t + 1) * 128, :], writes=[buf("%s%d" % (pre, j))])

        def norm_A(xb, pre, tiles=(0, 1, 2, 3)):
            for j in tiles:
                xb_j = buf("%s%d" % (pre, j))
                sc.op(act, lambda e, j=j: e.activation(out=junk[:, :], in_=xb[:, j, :], func=AF.Square, accum_out=ss[:, j:j + 1]),
                      reads=[xb_j], writes=[buf("sq"), buf("ss%d" % j)])
                sc.op(act, lambda e, j=j: e.activation(out=sstmp[:, j:j + 1], in_=ss[:, j:j + 1], func=AF.Ln, scale=1.0 / D, bias=epsc[:, 0:1]),
                      reads=[buf("ss%d" % j), buf("epsc")], writes=[buf("sstmp%d" % j)])
                sc.op(act, lambda e, j=j: e.activation(out=rstd[:, j:j + 1], in_=sstmp[:, j:j + 1], func=AF.Exp, scale=-0.5),
                      reads=[buf("sstmp%d" % j)], writes=[buf("rstd%d" % j)])
                sc.op(act, lambda e, j=j: e.activation(out=xn[:, j, :], in_=xb[:, j, :], func=AF.Identity, scale=rstd[:, j:j + 1]),
                      reads=[xb_j, buf("rstd%d" % j)], writes=[buf("xn%d" % j)])

        def norm_B(gs_off, sh_off):
            for j in range(4):
                for c in range(8):
                    sc.op(pe, lambda e, c=c, j=j: e.transpose(PSBF[c // 4][:, (c % 4) * 512 + j * 128:(c % 4) * 512 + (j + 1) * 128],
                                                             xn[:, j, c * 128:(c + 1) * 128], ident[:, :]),
                          reads=[buf("xn%d" % j), buf("ident")], writes=[PB[c // 2]], ms=(j == 3))
            for c in range(8):
                src = PSBF[c // 4][:, (c % 4) * 512:(c % 4 + 1) * 512]
                if (c // 2) % 2 == 0:
                    sc.op(dve, lambda e, c=c, src=src: e.tensor_scalar(out=hT[:, c, :], in0=src, scalar1=gsT[:, gs_off + c:gs_off + c + 1],
                                                                      scalar2=modT[:, sh_off + c:sh_off + c + 1], op0=ALU.mult, op1=ALU.add),
                          reads=[PB[c // 2], buf("gsT"), buf("modT")], writes=[buf("hT%d" % c)])
                else:
                    sc.op(act, lambda e, c=c, src=src: e.activation(out=hT[:, c, :], in_=src, func=AF.Identity,
                                                                   scale=gsT[:, gs_off + c:gs_off + c + 1],
                                                                   bias=modT[:, sh_off + c:sh_off + c + 1]),
                          reads=[PB[c // 2], buf("gsT"), buf("modT")], writes=[buf("hT%d" % c)])

        def norm_group(g, gs_off, sh_off, first):
            norm_A(xg, "xg")
            norm_B(gs_off, sh_off)

        def rms_feat(src_banks, nchunk, ones_idx, gcol, dst_fn, dst_buf):
            for i, pb in enumerate(src_banks):
                sc.op(act, lambda e, i=i, pb=pb: e.activation(out=sq[:, i, :], in_=bank(pb), func=AF.Square),
                      reads=[PB[pb]], writes=[buf("sq")])
            sb_ = 7
            for i in range(nchunk):
                sc.op(pe, lambda e, i=i: e.matmul(bank(sb_), ones_bf[:, ones_idx, :], sq[:, i, :], start=(i == 0), stop=(i == nchunk - 1)),
                      reads=[buf("sq"), buf("ones_bf")], writes=[PB[sb_]], ms=(i == nchunk - 1))
            sc.op(act, lambda e: e.activation(out=nrm_tmp[:, :], in_=bank(sb_), func=AF.Ln, bias=epsc[:, 0:1]),
                  reads=[PB[sb_], buf("epsc")], writes=[buf("nrm_tmp")])
            sc.op(act, lambda e: e.activation(out=nrm_rs[:, :], in_=nrm_tmp[:, :], func=AF.Exp, scale=-0.5),
                  reads=[buf("nrm_tmp")], writes=[buf("nrm_rs")])
            for i, pb in enumerate(src_banks):
                sc.op(dve, lambda e, i=i, pb=pb: e.scalar_tensor_tensor(out=dst_fn(i), in0=bank(pb), scalar=vecs[:, gcol + i:gcol + i + 1],
                                                                       in1=nrm_rs[:, :], op0=ALU.mult, op1=ALU.mult),
                      reads=[PB[pb], buf("vecs"), buf("nrm_rs")], writes=[dst_buf])

        def rope_evac(pbA, pbB, g, dst_ap, dst_buf):
            gs_ = slice(g * GT, (g + 1) * GT)
            sc.op(dve, lambda e: e.tensor_tensor(out=rp1[R, :], in0=bank_rows(pbA, 64, 96), in1=cosT[R, gs_], op=ALU.mult),
                  reads=[PB[pbA], buf("cs")], writes=[buf("nrm_tmp")])
            sc.op(dve, lambda e: e.tensor_tensor(out=rp2[R, :], in0=bank_rows(pbB, 64, 96), in1=sinT[R, gs_], op=ALU.mult),
                  reads=[PB[pbB], buf("cs")], writes=[buf("nrm_rs")])
            sc.op(dve, lambda e: e.tensor_tensor(out=dst_ap, in0=rp1[R, :], in1=rp2[R, :], op=ALU.add),
                  reads=[buf("nrm_tmp"), buf("nrm_rs")], writes=[dst_buf])

        hbufs = [buf("hT%d" % c_) for c_ in range(8)]
        def C_group(g):
            gsl = slice(g * GT, (g + 1) * GT)
            if g == 0:
                dump("hT0", lambda: hT[:, 0, :], [buf("hT0")])
                dump("hT7", lambda: hT[:, 7, :], [buf("hT7")])
            rhs_h = lambda k: hT[:, k, :]
            lin_T(4, win_v, C_KVA, 128, 8, rhs_h, hbufs)
            lin_T(5, None, 0, 32, 8, rhs_h, hbufs, out_rows=(64, 96), tile_pos=(0, 64),
                  wt=(lambda k: wkr[:, k, :], buf("wkr")))
            lin_T(6, None, 0, 32, 8, rhs_h, hbufs, out_rows=(64, 96), tile_pos=(0, 64),
                  wt=(lambda k: wkrot[:, k, :], buf("wkrot")))
            def glu(c):
                pu = next_bank([0, 1, 2, 3])
                lin_T(pu, win_v, C_U + c * 128, 128, 8, rhs_h, hbufs)
                pg = next_bank([0, 1, 2, 3])
                lin_T(pg, win_v, C_UG + c * 128, 128, 8, rhs_h, hbufs)
                sc.op(act, lambda e, pg=pg, c=c: e.activation(out=sig[:, c % 2, :], in_=bank(pg), func=AF.Sigmoid),
                      reads=[PB[pg]], writes=[buf("sig%d" % (c % 2))])
                sc.op(dve, lambda e, pu=pu, c=c, g=g: e.tensor_tensor(out=zT[:, c, 16 + g * GT:16 + (g + 1) * GT], in0=bank(pu),
                                                                     in1=sig[:, c % 2, :], op=ALU.mult),
                      reads=[PB[pu], buf("sig%d" % (c % 2))], writes=[buf("zT")])

            glu(0)
            glu(1)
            rms_feat([4], 1, 0, V_GKV, lambda i: kvnT[:, :], buf("kvnT"))
            rope_evac(5, 6, g, kT[R, 0, gsl], buf("kT"))
            for h in range(1, NH):
                sc.op((act if h % 2 else dve), lambda e, h=h, gsl=gsl: (e.activation(out=kT[R, h, gsl], in_=kT[R, 0, gsl], func=AF.Identity) if h % 2 else e.tensor_copy(out=kT[R, h, gsl], in_=kT[R, 0, gsl])), reads=[buf("kT")], writes=[buf("kT")])
            if g == 0:
                dump("kvnT", lambda: kvnT[:, :], [buf("kvnT")])
                dump("krope", lambda: kT[R, 0, 0:GT], [buf("kT")], rows=32)
            for h in range(NH):
                pb = next_bank([0, 1, 2, 3])
                sc.op(pe, lambda e, h=h, pb=pb: e.matmul(bank_rows(pb, 0, 64), wkv[:, h * 128:h * 128 + 64], kvnT[:, :], start=True, stop=True),
                      reads=[buf("wkv"), buf("kvnT")], writes=[PB[pb]])
                sc.op(act, lambda e, h=h, pb=pb, gsl=gsl: e.activation(out=kT[0:64, h, gsl], in_=bank_rows(pb, 0, 64), func=AF.Identity),
                      reads=[PB[pb]], writes=[buf("kT")])
            for j in range(4):
                t = g * 4 + j
                pb = next_bank([0, 1, 2, 3])
                sc.op(pe, lambda e, j=j, pb=pb: e.matmul(bank(pb), kvnT[:, j * 128:(j + 1) * 128], wv[:, :, :].rearrange("p h e -> p (h e)"),
                                                        start=True, stop=True),
                      reads=[buf("wv"), buf("kvnT")], writes=[PB[pb]])
                sc.op(dve, lambda e, t=t, pb=pb: e.tensor_copy(out=vaug[:, t, :, 0:64], in_=bank(pb).rearrange("p (h e) -> p h e", h=NH)),
                      reads=[PB[pb]], writes=[buf("vaug")])
            glu(2)
            glu(3)
            if g < NG - 1:
                load_x(g + 1)
            else:
                load_x(0)
            trig = [buf("zT").w]
            if g == 0:
                S_win.cast(3, None, after=trig)
            elif g == 1:
                mod_late(0, trig)
                S_wco.cast(after=trig)
                S_wao.cast(after=trig)
                S_wout.cast(after=trig)
            elif g == 2:
                mod_late(2, trig)
                S_wg.cast(after=trig)
            else:
                S_wu.cast(after=trig)
                S_wd.cast(after=trig)
            if g == 0:
                dump("zT0", lambda: zT[:, 0, 16:16 + GT], [buf("zT")])
                dump("kn0", lambda: kT[0:64, 0, 0:GT], [buf("kT")], rows=64)
                dump("v0", lambda: vaug[:, 0, 0:4, :].rearrange("p h e -> p (h e)"), [buf("vaug")], cols=4 * 66)


        def C2_group(g):
            gsl = slice(g * GT, (g + 1) * GT)
            for c in range(4):
                pb = c
                for k in range(31):
                    sc.op(pe, lambda e, c=c, k=k, pb=pb, g=g: e.matmul(bank(pb), dg[:, c, k, :], zT[:, c, g * GT + k + 1:g * GT + k + 1 + GT],
                                                                      start=(k == 0), stop=(k == 30)),
                          reads=[buf("dg%d" % c), buf("zT")], writes=[PB[pb]], ms=(k == 30))
                sc.op(act, lambda e, c=c, pb=pb: e.activation(out=zc[:, c, :], in_=bank(pb), func=AF.Identity,
                                                             bias=vecs[:, V_BDW + c:V_BDW + c + 1]),
                      reads=[PB[pb], buf("vecs")], writes=[buf("zc")])
                sc.op(dve, lambda e, c=c: e.tensor_copy(out=zcb[:, c, :], in_=zc[:, c, :]), reads=[buf("zc")], writes=[buf("zcb")])
                sc.op(act, lambda e, c=c: e.activation(out=sq[:, c, :], in_=zc[:, c, :], func=AF.Square), reads=[buf("zc")], writes=[buf("sq")])
            for c in range(4):
                sc.op(pe, lambda e, c=c: e.matmul(bank(4), ones_bf[:, 2, :], zcb[:, c, :], start=(c == 0), stop=(c == 3)),
                      reads=[buf("zcb"), buf("ones_bf")], writes=[PB[4]], ms=(c == 3))
            for c in range(4):
                sc.op(pe, lambda e, c=c: e.matmul(bank(5), ones_bf[:, 2, :], sq[:, c, :], start=(c == 0), stop=(c == 3)),
                      reads=[buf("sq"), buf("ones_bf")], writes=[PB[5]], ms=(c == 3))
            sc.op(act, lambda e: e.activation(out=mean_sb[:, :], in_=bank(4), func=AF.Identity), reads=[PB[4]], writes=[buf("mean_sb")])
            sc.op(dve, lambda e: e.tensor_tensor(out=m2[:, :], in0=mean_sb[:, :], in1=mean_sb[:, :], op=ALU.mult),
                  reads=[buf("mean_sb")], writes=[buf("m2")])
            sc.op(dve, lambda e: e.scalar_tensor_tensor(out=nrm_tmp[:, :], in0=bank(5), scalar=EPS_LN, in1=m2[:, :], op0=ALU.add, op1=ALU.subtract),
                  reads=[PB[5], buf("m2")], writes=[buf("nrm_tmp")])
            sc.op(act, lambda e: e.activation(out=nrm_tmp[:, :], in_=nrm_tmp[:, :], func=AF.Ln),
                  reads=[buf("nrm_tmp")], writes=[buf("nrm_tmp")])
            sc.op(act, lambda e: e.activation(out=nrm_rs[:, :], in_=nrm_tmp[:, :], func=AF.Exp, scale=-0.5),
                  reads=[buf("nrm_tmp")], writes=[buf("nrm_rs")])
            for c in range(4):
                sc.op(dve, lambda e, c=c: e.tensor_tensor(out=zc[:, c, :], in0=zc[:, c, :], in1=mean_sb[:, :], op=ALU.subtract),
                      reads=[buf("zc"), buf("mean_sb")], writes=[buf("zc")])
                sc.op(dve, lambda e, c=c: e.scalar_tensor_tensor(out=zc[:, c, :], in0=zc[:, c, :], scalar=vecs[:, V_GLN + c:V_GLN + c + 1],
                                                                in1=nrm_rs[:, :], op0=ALU.mult, op1=ALU.mult),
                      reads=[buf("zc"), buf("nrm_rs"), buf("vecs")], writes=[buf("zc")])
                sc.op(act, lambda e, c=c, gsl=gsl: e.activation(out=zsT[:, c, gsl], in_=zc[:, c, :], func=AF.Silu,
                                                               bias=vecs[:, V_BLN + c:V_BLN + c + 1]),
                      reads=[buf("zc"), buf("vecs")], writes=[buf("zT")])
            if g == 0:
                dump("zs0", lambda: zsT[:, 0, 0:GT], [buf("zT")])

        rhs_hq = lambda k: hT[:, k, :]
        rhs_q = lambda k: qanT[:, k, :]

        def q_pre_A():
            lin_T(4, win_v, C_QA, 128, 8, rhs_hq, hbufs)
            lin_T(5, win_v, C_QA + 128, 128, 8, rhs_hq, hbufs)
            rms_feat([4, 5], 2, 1, V_GQA, lambda i: qanT[:, i, :], buf("qanT"))

        load_x(0)
        norm_A(xg, "xg")
        norm_B(0, 0)
        C_group(0)
        norm_A(xg, "xg")
        norm_B(0, 0)
        C_group(1)
        norm_A(xg, "xg")
        barrier()
        C2_group(0)
        mod_late(1)
        norm_B(0, 0)
        C_group(2)
        norm_A(xg, "xg")
        C2_group(1)
        mod_late(3)
        norm_B(0, 0)
        C_group(3)
        norm_A(xg, "xg")
        C2_group(2)
        norm_B(0, 0)
        q_pre_A()
        C2_group(3)
        checkpoint('C2')
        barrier()
        for i_ in range(2):
            sc.dma(pool, wq[:, :, i_ * 384:(i_ + 1) * 384], wq_d.rearrange("(k p) n -> p k n", p=128)[:, :, i_ * 384:(i_ + 1) * 384], writes=[buf("wq")])
        sc.dma(pool, wqrot, wqrot_d.rearrange("(k p) n -> p k n", p=128), writes=[buf("wqrot")])
        wqr4 = wqrot.rearrange("p k (h e) -> p k h e", h=NH)
        sc.op(pool, lambda e: e.tensor_scalar(out=wqr4[:, :, :, 0:16], in0=wqr4[:, :, :, 0:16], scalar1=-1.0, scalar2=None, op0=ALU.mult),
              reads=[buf("wqrot")], writes=[buf("wqrot")])
        def q_head(h, gg):
            pa, pb2 = 6, 7
            lin_T(pa, None, 0, 96, 2, rhs_q, [buf("qanT")], wt=(lambda k, h=h: wq[:, k, h * 96:(h + 1) * 96], buf("wq")))
            lin_T(pb2, None, 0, 32, 2, rhs_q, [buf("qanT")], out_rows=(64, 96), tile_pos=(0, 64),
                  wt=(lambda k, h=h: wqrot[:, k, h * 32:(h + 1) * 32], buf("wqrot")))
            sc.op(dve, lambda e, h=h, pa=pa: e.tensor_copy(out=qT[0:64, h, :], in_=bank_rows(pa, 0, 64)),
                  reads=[PB[pa]], writes=[buf("qT%d" % h)])
            rope_evac(pa, pb2, gg, qT[R, h, :], buf("qT%d" % h))

        XB = [(xg, "xg"), (xgB, "xh")]
        for g in range(NG):
            gsl = slice(g * GT, (g + 1) * GT)
            xb, pre = XB[g % 2]
            xb_n, pre_n = XB[(g + 1) % 2]
            if g == 0:
                pass
            rhs_h = lambda k: hT[:, k, :]
            if g == 0:
                q_head(0, 0)
            if g == 0:
                dump("qT0", lambda: qT[0:96, 0, :], [buf("qT0")], rows=96)
            checkpoint('Dq')
            def S_pair(h, kp):
                pbs = (0, 1) if kp % 2 == 0 else (2, 3)
                for i in range(2):
                    kt = kp * 2 + i
                    sc.op(pe, lambda e, h=h, kt=kt, pb=pbs[i]: e.matmul(bank(pb), kT[0:96, h, kt * 128:(kt + 1) * 128], qT[0:96, h, :],
                                                                       start=True, stop=True),
                          reads=[buf("kT"), buf("qT%d" % h)], writes=[PB[pbs[i]]], ms=(i == 1))

            def E_pair(h, kp):
                pbs = (0, 1) if kp % 2 == 0 else (2, 3)
                pi_ = (h * 8 + kp) % NPT
                pt = pT[pi_]
                sc.op(act, lambda e, pt=pt, j=pbs[0] // 2: e.activation(out=pt.rearrange("p a n -> p (a n)"), in_=PSB[j][:, :],
                                                                       func=AF.Exp, scale=ATTN_SCALE),
                      reads=[PB[pbs[0]], PB[pbs[1]]], writes=[buf("pT%d" % pi_)])

            def P_pair(h, kp):
                po = 4 + (h % 2)
                pi_ = (h * 8 + kp) % NPT
                pt = pT[pi_]
                ptb = buf("pT%d" % pi_)
                for i in range(2):
                    kt = kp * 2 + i
                    sc.op(pe, lambda e, h=h, kt=kt, i=i, pt=pt, po=po: e.matmul(bank_rows(po, 0, 66), vaug[:, kt, h, :], pt[:, i, :],
                                                                               start=(kt == 0), stop=(kt == 15)),
                          reads=[buf("vaug"), ptb], writes=[PB[po]], ms=(i == 1))

            def tail_a(h):
                po = 4 + (h % 2)
                sc.op(act, lambda e, po=po: e.activation(out=rden[64:65, :], in_=bank_rows(po, 64, 65), func=AF.Ln), reads=[PB[po]], writes=[buf("evt")])
                sc.op(act, lambda e: e.activation(out=rden[64:65, :], in_=rden[64:65, :], func=AF.Exp, scale=-1.0), reads=[buf("evt")], writes=[buf("evt")])

            def tail_b(h):
                po = 4 + (h % 2)
                pr = 6 + (h % 2)
                sc.op(pe, lambda e, pr=pr: e.matmul(bank_rows(pr, 0, 64), ones_f[64:65, 0:64], rden[64:65, :], start=True, stop=True),
                      reads=[buf("evt"), buf("ones_f")], writes=[PB[pr]])
                sc.op(dve, lambda e, pr=pr: e.tensor_copy(out=rb[:, :], in_=bank_rows(pr, 0, 64)), reads=[PB[pr]], writes=[buf("nrm_tmp")])
                sc.op(dve, lambda e, h=h, po=po: e.tensor_tensor(out=aoT[:, h, :], in0=bank_rows(po, 0, 64), in1=rb[:, :], op=ALU.mult),
                      reads=[PB[po], buf("nrm_tmp")], writes=[buf("aoT")])

            S_pair(0, 0)
            pending = None
            for it in range(NH * 8):
                h, kp = divmod(it, 8)
                if it + 1 < NH * 8:
                    S_pair(*divmod(it + 1, 8))
                E_pair(h, kp)
                if it >= 1:
                    hp, kpp = divmod(it - 1, 8)
                    P_pair(hp, kpp)
                    if kpp == 7:
                        pending = hp
                if kp == 1 and pending is not None:
                    tail_a(pending)
                if kp == 2 and pending is not None:
                    tail_b(pending)
                    pending = None
                if kp == 4 and h + 1 < NH:
                    q_head(h + 1, g)
            P_pair(NH - 1, 7)
            tail_a(NH - 1)
            tail_b(NH - 1)
            if g == 0:
                dump("ao0", lambda: aoT[:, 0, :], [buf("aoT")], rows=64)
            checkpoint('Dattn')
            if g + 1 < NG:
                load_x(g + 1, xb_n, pre_n)
            for c in range(8):
                pya, pga, pgb, pyb = (0, 1, 2, 3) if c % 2 == 0 else (4, 5, 6, 7)
                iw = st["w"] % NW
                st["w"] += 1
                ap_, sbuf_ = S_wao.get(c)
                sc.dma(sp, wbuf[iw][0:64, :, :], ap_, reads=[sbuf_], writes=[buf("wbuf%d" % iw)])
                for h in range(NH):
                    sc.op(pe, lambda e, h=h, iw=iw, pya=pya: e.matmul(bank(pya), wbuf[iw][0:64, h, :], aoT[:, h, :],
                                                                     start=(h == 0), stop=(h == NH - 1)),
                          reads=[buf("wbuf%d" % iw), buf("aoT")], writes=[PB[pya]], ms=(h == NH - 1))
                lin_T(pga, win_v, C_GA + c * 128, 128, 8, rhs_h, hbufs)
                lin_T(pgb, win_v, C_GB + c * 128, 128, 8, rhs_h, hbufs)
                lin_T(pyb, wco_v, c * 128, 128, 4, lambda k, gsl=gsl: zsT[:, k, gsl], [buf("zT")])
                sc.op(act, lambda e, pga=pga: e.activation(out=sig[:, 0, :], in_=bank(pga), func=AF.Sigmoid), reads=[PB[pga]], writes=[buf("sig0")])
                sc.op(act, lambda e, pgb=pgb: e.activation(out=sig[:, 1, :], in_=bank(pgb), func=AF.Sigmoid), reads=[PB[pgb]], writes=[buf("sig1")])
                sc.op(dve, lambda e, pya=pya: e.tensor_tensor(out=sig[:, 0, :], in0=bank(pya), in1=sig[:, 0, :], op=ALU.mult),
                      reads=[PB[pya], buf("sig0")], writes=[buf("sig0")])
                sc.op(dve, lambda e, pyb=pyb: e.tensor_tensor(out=sig[:, 1, :], in0=bank(pyb), in1=sig[:, 1, :], op=ALU.mult),
                      reads=[PB[pyb], buf("sig1")], writes=[buf("sig1")])
                sc.op(dve, lambda e, c=c: e.tensor_tensor(out=mT[:, c, :], in0=sig[:, 0, :], in1=sig[:, 1, :], op=ALU.add),
                      reads=[buf("sig0"), buf("sig1")], writes=[buf("actT")])
            if g == 0:
                dump("mT0", lambda: mT[:, 0, :], [buf("actT")])
            checkpoint('Dmerge')
            def tok_proj(W_d, row0, nk, lhs_fn, lhs_bufs, gate_off, xb=xb, pre=pre, tile_groups=((0, 1, 2, 3),), hook=None, hf_outer=False):
                for tgi, tg in enumerate(tile_groups):
                    if tgi == 1 and hook is not None:
                        hook()
                    nt = len(tg)
                    bank_of = {}
                    for hf in range(2):
                        for jj, j in enumerate(tg):
                            bank_of[(hf, j)] = (hf * 4 + j) if nt == 4 else ((j // 2) * 4 + hf * 2 + jj)
                    for k in range(nk):
                        i = st["w"] % NW
                        st["w"] += 1
                        wfl = wbuf[i][:, :, :].rearrange("p k n -> p (k n)")
                        ap_, sbuf_ = W_d.get(row0 // 128 + k)
                        sc.dma(sp, wfl, ap_, reads=[sbuf_], writes=[buf("wbuf%d" % i)])
                        for hf in range(2):
                            for jj, j in enumerate(tg):
                                pb = bank_of[(hf, j)]
                                last = (hf == 1 and jj == nt - 1)
                                sc.op(pe, lambda e, wfl=wfl, k=k, j=j, hf=hf, pb=pb: e.matmul(bank(pb), lhs_fn(k, j), wfl[:, hf * 512:(hf + 1) * 512],
                                                                                            start=(k == 0), stop=(k == nk - 1)),
                                      reads=[buf("wbuf%d" % i)] + lhs_bufs, writes=[PB[pb]], ms=(k == nk - 1 or last))
                    order = [(j, hf) for hf in range(2) for j in tg] if hf_outer else [(j, hf) for j in tg for hf in range(2)]
                    for ei, (j, hf) in enumerate(order):
                        if True:
                            pb = bank_of[(hf, j)]
                            ev_, evb_ = (evt, buf("evt")) if ei % 2 == 0 else (nrm_rs, buf("nrm_rs"))
                            sc.op(dve, lambda e, pb=pb, hf=hf, ev_=ev_: e.tensor_tensor(out=ev_[:, :], in0=bank(pb),
                                                                                       in1=gate_bc[:, gate_off + hf * 512:gate_off + (hf + 1) * 512], op=ALU.mult),
                                  reads=[PB[pb], buf("gate_bc")], writes=[evb_])
                            sc.op(dve, lambda e, j=j, hf=hf, ev_=ev_: e.tensor_tensor(out=xb[:, j, hf * 512:(hf + 1) * 512], in0=xb[:, j, hf * 512:(hf + 1) * 512],
                                                                                     in1=ev_[:, :], op=ALU.add),
                                  reads=[buf("%s%d" % (pre, j)), evb_], writes=[buf("%s%d" % (pre, j))])

            tok_proj(S_wout, 0, 8, lambda k, j: mT[:, k, j * 128:(j + 1) * 128], [buf("actT")], 0, tile_groups=((0, 1), (2, 3)))
            if g == 0:
                dump("x1", lambda: xg[:, 0, 0:GT], [buf("xg0")])
            checkpoint('Dx1')
            norm_A(xb, pre)
            norm_B(8, 24)
            for part in range(2):
                for fi in range(NFH):
                    f = part * NFH + fi
                    if part == 0 and g + 1 < NG and fi in (2, 4, 6, 8):
                        norm_A(xb_n, pre_n, tiles=((fi - 2) // 2,))
                    pg_, pu_ = (0, 1) if f % 2 == 0 else (2, 3)
                    lin_T(pg_, wg_v, f * 128, 128, 8, rhs_h, hbufs)
                    lin_T(pu_, wu_v, f * 128, 128, 8, rhs_h, hbufs)
                    sc.op(act, lambda e, pg_=pg_, f=f: e.activation(out=sig[:, f % 2, :], in_=bank(pg_), func=AF.Silu),
                          reads=[PB[pg_]], writes=[buf("sig%d" % (f % 2))])
                    sc.op(dve, lambda e, pu_=pu_, f=f, fi=fi: e.tensor_tensor(out=actT[:, fi, :], in0=bank(pu_), in1=sig[:, f % 2, :], op=ALU.mult),
                          reads=[PB[pu_], buf("sig%d" % (f % 2))], writes=[buf("actT")])
                if part == 1 and g + 1 < NG:
                    norm_B(0, 0)
                    q_pre_A()
                    tok_proj(S_wd, part * NFH * 128, NFH, lambda k, j: actT[:, k, j * 128:(j + 1) * 128], [buf("actT")], D,
                             tile_groups=((0, 1), (2, 3)), hook=lambda g=g: q_head(0, g + 1))
                else:
                    tok_proj(S_wd, part * NFH * 128, NFH, lambda k, j: actT[:, k, j * 128:(j + 1) * 128], [buf("actT")], D, hf_outer=(part == 0))
            for j in range(4):
                t = g * 4 + j
                xb_j = buf("%s%d" % (pre, j))
                sc.op(act, lambda e, j=j, xb=xb: e.activation(out=junk[:, :], in_=xb[:, j, :], func=AF.Square, accum_out=ss[:, j:j + 1]),
                      reads=[xb_j], writes=[buf("sq"), buf("ss%d" % j)])
                sc.op(act, lambda e, j=j: e.activation(out=sstmp[:, j:j + 1], in_=ss[:, j:j + 1], func=AF.Ln, scale=1.0 / D, bias=epsc[:, 0:1]),
                      reads=[buf("ss%d" % j), buf("epsc")], writes=[buf("sstmp%d" % j)])
                sc.op(act, lambda e, j=j: e.activation(out=rstd[:, j:j + 1], in_=sstmp[:, j:j + 1], func=AF.Exp, scale=-0.5),
                      reads=[buf("sstmp%d" % j)], writes=[buf("rstd%d" % j)])
                sc.op(dve, lambda e, j=j, xb=xb: e.scalar_tensor_tensor(out=xb[:, j, :], in0=xb[:, j, :], scalar=rstd[:, j:j + 1], in1=gfin_bc[:, :],
                                                                       op0=ALU.mult, op1=ALU.mult),
                      reads=[xb_j, buf("rstd%d" % j), buf("gate_bc")], writes=[xb_j])
                out_tickets.append(sc.dma(pool, out_d[t * 128:(t + 1) * 128, :], xb[:, j, :], reads=[xb_j], writes=[buf("outd%d" % t)]))

    try:
        emit_all()
    except StopBuild:
        pass

    if dbg and buf("dbg_out").w is not None:
        out_tickets.append(buf("dbg_out").w)
        for k_, v_ in buf("dbg_out").r.items():
            out_tickets.append(v_)
    sc.wait_all(sp, out_tickets)

    with contextlib.ExitStack() as stack:
        sc.alloc_sems(stack)
        block = stack.enter_context(nc.Block())

        @block.tensor
        def _(e):
            for th in sc.q["pe"]:
                th(e)

        @block.scalar
        def _(e):
            for th in sc.q["act"]:
                th(e)

        @block.vector
        def _(e):
            for th in sc.q["dve"]:
                th(e)

        @block.gpsimd
        def _(e):
            for th in sc.q["pool"]:
                th(e)

        @block.sync
        def _(e):
            for th in sc.q["sp"]:
                th(e)
    return nc, sc


WIN_COLS = ([C_KVA] + [C_U + 128 * i for i in range(4)] + [C_UG + 128 * i for i in range(4)] + [C_QA, C_QA + 128]
            + [C_GA + 128 * i for i in range(8)] + [C_GB + 128 * i for i in range(8)])


def chunk_cols(w, col_starts):
    K = w.shape[0]
    out = np.empty((len(col_starts), 128, K // 128, 128), dtype=w.dtype)
    for i, c0 in enumerate(col_starts):
        out[i] = w[:, c0:c0 + 128].reshape(K // 128, 128, 128).transpose(1, 0, 2)
    return out


def host_inputs(b, inp):
    f = np.float32
    w_in = np.ascontiguousarray(inp["w_in"][0], dtype=f)
    w_q = np.ascontiguousarray(inp["w_q_up"][0], dtype=f)
    wq3 = w_q.reshape(256, NH, 96)
    w_q_rot = np.ascontiguousarray(np.concatenate([wq3[:, :, 80:96], wq3[:, :, 64:80]], axis=2).reshape(256, NH * 32))
    kr = w_in[:, C_KR:C_KR + 32]
    w_k_rot = np.ascontiguousarray(np.concatenate([kr[:, 16:32], kr[:, 0:16]], axis=1))

    def colv(v, n):
        return np.asarray(v, dtype=f).reshape(n, 128).T

    vecs = np.zeros((128, NVEC), dtype=f)
    vecs[:, V_GMIX:V_GMIX + 8] = colv(inp["g_norm_mix"][0], 8)
    vecs[:, V_GFFN:V_GFFN + 8] = colv(inp["g_norm_ffn"][0], 8)
    vecs[:, V_GQA:V_GQA + 2] = colv(inp["g_q_a"][0], 2)
    vecs[:, V_GKV:V_GKV + 1] = colv(inp["g_kv_a"][0], 1)
    vecs[:, V_BDW:V_BDW + 4] = colv(inp["b_dw"][0], 4)
    vecs[:, V_GLN:V_GLN + 4] = colv(inp["g_conv_ln"][0], 4)
    vecs[:, V_BLN:V_BLN + 4] = colv(inp["b_conv_ln"][0], 4)
    wdw = np.asarray(inp["w_dw"][0], dtype=f)
    vecs[:, V_WDW:V_WDW + 124] = wdw.reshape(31, 4, 128).transpose(2, 1, 0).reshape(128, 124)
    inv_freq = (10000.0 ** (-np.arange(0, 32, 2, dtype=np.float32) / np.float32(32))).astype(f)
    vecs[64:96, V_INVF] = np.concatenate([inv_freq, inv_freq])
    vecs[:, V_C:V_C + 8] = colv(inp["c"][b], 8)
    vecs[:, V_BADA:V_BADA + 48] = colv(inp["b_ada"][0], 48)
    vecs[:, V_GFIN:V_GFIN + 8] = colv(inp["g_final"], 8)
    return {
        "x": np.ascontiguousarray(inp["x"][b], dtype=f),
        "pos": np.ascontiguousarray(np.asarray(inp["positions"][b]).reshape(1, S).astype(np.int32)),
        "w_ada": np.ascontiguousarray(inp["w_ada"][0], dtype=f),
        "w_in_c": chunk_cols(w_in, WIN_COLS),
        "w_k_r": np.ascontiguousarray(w_in[:, C_KR:C_KR + 32]),
        "w_q_up": w_q,
        "w_q_rot": w_q_rot,
        "w_k_rot": w_k_rot,
        "w_kv_up": np.ascontiguousarray(inp["w_kv_up"][0], dtype=f),
        "w_attn_o_c": np.ascontiguousarray(np.asarray(inp["w_attn_o"][0], dtype=f).reshape(NH, 64, 8, 128).transpose(2, 1, 0, 3)),
        "w_conv_out_c": chunk_cols(np.asarray(inp["w_conv_out"][0], dtype=f), [c_ * 128 for c_ in range(8)]),
        "w_out": np.ascontiguousarray(np.asarray(inp["w_out"][0], dtype=f).reshape(8, 128, D)),
        "w_ffn_gate_c": chunk_cols(np.asarray(inp["w_ffn_gate"][0], dtype=f), [c_ * 128 for c_ in range(NFF)]),
        "w_ffn_up_c": chunk_cols(np.asarray(inp["w_ffn_up"][0], dtype=f), [c_ * 128 for c_ in range(NFF)]),
        "w_ffn_down": np.ascontiguousarray(np.asarray(inp["w_ffn_down"][0], dtype=f).reshape(NFF, 128, D)),
        "vecs": vecs,
        "ident": np.eye(128, dtype=f),
    }


def kernel(**inputs):
    inp = {k: np.asarray(v) for k, v in inputs.items()}
    nc, _ = build_nc()
    in_maps = [host_inputs(b, inp) for b in range(NCORES)]
    res = run_bass_kernel_spmd(nc, in_maps, core_ids=list(range(NCORES)))
    out = np.stack([np.asarray(res.results[b]["out"], dtype=np.float32) for b in range(NCORES)], axis=0)
    return out
```

```python
import math
import numpy as np
import concourse.bass as bass
import concourse.mybir as mybir
from concourse.bass_utils import run_bass_kernel_spmd

F32 = mybir.dt.float32
BF16 = mybir.dt.bfloat16
I32 = mybir.dt.int32
ALU = mybir.AluOpType
AF = mybir.ActivationFunctionType

D = 1024
S = 2048
NCORES = 8
NH = 8
DFF = 2816
NFF = DFF // 128
DIN = 3488
GT = 512
NG = S // GT
EPS_RMS = 1e-6
EPS_LN = 1e-5
ATTN_SCALE = 1.0 / math.sqrt(96.0)
C_QA, C_KVA, C_KR, C_U, C_UG, C_GA, C_GB = 0, 256, 384, 416, 928, 1440, 2464
V_GMIX, V_GFFN, V_GQA, V_GKV, V_BDW, V_GLN, V_BLN, V_WDW, V_INVF, V_C = 0, 8, 16, 18, 19, 23, 27, 31, 155, 156
V_BADA, V_GFIN = 164, 212
NVEC = 220
TWO_PI = 2.0 * math.pi
RC1 = 6.28125
RC2 = TWO_PI - RC1
PI_CL = 3.1415925


class StopBuild(Exception):
    pass


class Buf:
    __slots__ = ("w", "r", "name")

    def __init__(self, name=""):
        self.w = None
        self.r = {}
        self.name = name


class Sched:
    ENGS = ("pe", "act", "dve", "pool", "sp")

    def __init__(self, nc, ndma=32):
        self.nc = nc
        self.q = {e: [] for e in self.ENGS}
        self.cnt = {e: 0 for e in self.ENGS}
        self.seen = {e: {} for e in self.ENGS}
        self.sems = {}
        self.ndma = ndma
        self.dcnt = [0] * ndma
        self.dnext = 0
        self.dnext_q = {"sp": 0, "pool": 0}
        self.ninst = {e: 0 for e in self.ENGS}

    def alloc_sems(self, stack):
        for e in ("pe", "act", "dve", "pool"):
            self.sems[e] = stack.enter_context(self.nc.semaphore("s_" + e))
        for i in range(self.ndma):
            self.sems[("d", i)] = stack.enter_context(self.nc.semaphore("s_d%d" % i))

    def _waits(self, eng, deps):
        best = {}
        for d in deps:
            if d is None:
                continue
            k, v = d
            if k == eng == "pe":
                continue
            if v > best.get(k, 0):
                best[k] = v
        for k, v in best.items():
            if self.seen[eng].get(k, 0) >= v:
                continue
            self.seen[eng][k] = v
            self.q[eng].append(lambda e, k=k, v=v: e.wait_ge(self.sems[k], v))

    def op(self, eng, fn, reads=(), writes=(), ms=True):
        deps = []
        for b in reads:
            deps.append(b.w)
        for b in writes:
            deps.append(b.w)
            deps.extend(b.r.values())
        self._waits(eng, deps)
        if ms:
            self.cnt[eng] += 1
            tk = (eng, self.cnt[eng])
            self.q[eng].append(lambda e, fn=fn, eng=eng: fn(e).then_inc(self.sems[eng], 1))
        else:
            tk = (eng, self.cnt[eng] + 1)
            self.q[eng].append(lambda e, fn=fn: fn(e))
        self.ninst[eng] += 1
        for b in reads:
            b.r[eng] = tk
        for b in writes:
            b.w = tk
            b.r = {}
        return tk

    def dma(self, eng, out, in_, reads=(), writes=(), after=()):
        lo, n = (0, 12) if eng == "sp" else (12, self.ndma - 12)
        i = lo + self.dnext_q[eng] % n
        self.dnext_q[eng] += 1
        key = ("d", i)
        deps = list(after)
        if self.dcnt[i] > 0:
            deps.append((key, self.dcnt[i]))
        for b in reads:
            deps.append(b.w)
        for b in writes:
            deps.append(b.w)
            deps.extend(b.r.values())
        self._waits(eng, deps)
        self.dcnt[i] += 16
        tk = (key, self.dcnt[i])
        self.q[eng].append(lambda e, out=out, in_=in_, key=key: e.dma_start(out=out, in_=in_).then_inc(self.sems[key], 16))
        self.ninst[eng] += 1
        for b in reads:
            b.r[key] = tk
        for b in writes:
            b.w = tk
            b.r = {}
        return tk

    def wait_all(self, eng, tickets):
        self._waits(eng, tickets)


def build_nc(dbg=None):
    import contextlib
    nc = bass.Bass("TRN2", target_bir_lowering=False)
    dt_in = {}

    def din(name, shape, dtype=F32):
        dt_in[name] = nc.dram_tensor(name, list(shape), dtype, kind="ExternalInput")
        return dt_in[name].ap()

    x_d = din("x", [S, D])
    pos_h = nc.dram_tensor("pos", [1, S], I32, kind="ExternalInput")
    wada_d = din("w_ada", [D, 6 * D])
    win_d = din("w_in_c", [27, 128, 8, 128])
    wkr_d = din("w_k_r", [D, 32])
    wq_d = din("w_q_up", [256, 768])
    wqrot_d = din("w_q_rot", [256, 256])
    wkrot_d = din("w_k_rot", [D, 32])
    wkv_d = din("w_kv_up", [128, 1024])
    wao_d = din("w_attn_o_c", [8, 64, NH, 128])
    wco_d = din("w_conv_out_c", [8, 128, 4, 128])
    wout_d = din("w_out", [8, 128, D])
    wg_d = din("w_ffn_gate_c", [NFF, 128, 8, 128])
    wu_d = din("w_ffn_up_c", [NFF, 128, 8, 128])
    wd_d = din("w_ffn_down", [NFF, 128, D])
    vecs_d = din("vecs", [128, NVEC])
    ident_d = din("ident", [128, 128])
    out_d = nc.dram_tensor("out", [S, D], F32, kind="ExternalOutput").ap()
    dbg_d = None
    if dbg:
        dbg_d = nc.dram_tensor("dbg", [128, dbg["cols"]], F32, kind="ExternalOutput").ap()

    def sb(name, shape, dtype):
        return nc.alloc_sbuf_tensor(name, list(shape), dtype)

    kT = sb("kT", [128, NH, S], BF16)
    vaug = sb("vaug", [128, 16, NH, 66], BF16)
    zz = sb("zz", [128, 4, S + 32], BF16)
    zT = zz
    zsT = zz
    cosT = sb("cosT", [128, S], BF16)
    sinT = sb("sinT", [128, S], BF16)
    vecs = sb("vecs_sb", [128, NVEC], F32)
    ident = sb("ident_sb", [128, 128], BF16)
    ones_bf = sb("ones_bf", [128, 4, 128], BF16)
    ones_f = sb("ones_f", [128, 128], F32)
    epsc = sb("epsc", [128, 2], F32)
    modT = sb("modT", [128, 48], F32)
    gsT = sb("gsT", [128, 16], F32)
    caT = sb("caT", [128, 8], BF16)
    gate_bc = sb("gate_bc", [128, 2 * D], F32)
    gfin_bc = sb("gfin_bc", [128, D], F32)
    wkv = sb("wkv", [128, 1024], BF16)
    wv = sb("wv", [128, 8, 64], BF16)
    wkrot = sb("wkrot", [128, 8, 32], BF16)
    wkr = sb("wkr", [128, 8, 32], BF16)
    U2 = sb("U2", [128, 10240], BF16)
    U2f = U2.bitcast(F32)
    zc = U2f[:, 0:2048].rearrange("p (c n) -> p c n", c=4)
    zcb = U2[:, 4096:6144].rearrange("p (c n) -> p c n", c=4)
    mean_sb = U2f[:, 3072:3584]
    m2 = U2f[:, 3584:4096]
    xgB = U2f[:, 0:4096].rearrange("p (j n) -> p j n", j=4)
    wq = U2[:, 8192:9728].rearrange("p (k n) -> p k n", k=2)
    wqrot = U2[:, 9728:10240].rearrange("p (k n) -> p k n", k=2)
    NW = 6
    wbuf = [sb("wbuf%d" % i, [128, 8, 128], BF16) for i in range(NW)]
    xg = sb("xg", [128, 4, D], F32)
    xn = sb("xn", [128, 4, D], BF16)
    ss = sb("ss", [128, 4], F32)
    sstmp = sb("sstmp", [128, 4], F32)
    rstd = sb("rstd", [128, 4], F32)
    hT = sb("hT", [128, 8, GT], BF16)
    sq = sb("sq", [128, 4, GT], BF16)
    junk = sq[:, :, :].rearrange("p a n -> p (a n)")[:, 0:D]
    nrm_tmp = sb("nrm_tmp", [128, GT], F32)
    nrm_rs = sb("nrm_rs", [128, GT], F32)
    kvnT = sb("kvnT", [128, GT], BF16)
    qanT = sb("qanT", [128, 2, GT], BF16)
    rp1 = nrm_tmp
    rp2 = nrm_rs
    sig = sb("sig", [128, 2, GT], F32)
    U1 = sb("U1", [128, 16896], BF16)
    U1f = U1.bitcast(F32)
    xgf = xg[:, :, :].rearrange("p j n -> p (j n)")
    rop = U2f[:, 0:4096].rearrange("p (i n) -> p i n", i=4)
    ropi = U2.bitcast(I32)[:, 0:4096].rearrange("p (i n) -> p i n", i=4)
    dgt = U1f[:, 7936:8448].rearrange("p (i n) -> p i n", i=4)
    dg = U1[:, 0:15872].rearrange("p (c k m) -> p c k m", c=4, k=31)
    NFH = NFF // 2
    actT = U1[:, 0:5632].rearrange("p (f n) -> p f n", f=NFH)
    mT = U1[:, 0:4096].rearrange("p (c n) -> p c n", c=8)
    qT = U1[:, 5632:9728].rearrange("p (h n) -> p h n", h=NH)
    NPT = 2
    pT = [U1[:, 9728:10752].rearrange("p (a n) -> p a n", a=2), U1[:, 10752:11776].rearrange("p (a n) -> p a n", a=2)]
    aoT = U1[0:64, 11776:15872].rearrange("p (h n) -> p h n", h=NH)
    evt = U1f[:, 7936:8448]
    rden = evt
    rb = nrm_tmp[0:64, :]
    dbgt = sb("dbgt", [128, GT], F32) if dbg else None

    PSB = [nc.alloc_psum_tensor("psb%d" % i, [128, 1024], F32) for i in range(4)]

    def bank(b):
        return PSB[b // 2][:, (b % 2) * 512:(b % 2) * 512 + 512]

    def bank_rows(b, r0, r1):
        return PSB[b // 2][r0:r1, (b % 2) * 512:(b % 2) * 512 + 512]

    PSBF = [p.bitcast(BF16) for p in PSB]

    sc = Sched(nc)
    B = {}

    def buf(name):
        if name not in B:
            B[name] = Buf(name)
        return B[name]

    PB = [buf("bank%d" % i) for i in range(8)]

    pe, act, dve, pool, sp = "pe", "act", "dve", "pool", "sp"

    class Scr:
        def __init__(self, name, src, shape, step):
            self.name, self.src, self.step = name, src, step
            self.scr = nc.dram_tensor("scr_" + name, list(shape), BF16).ap()
            self.n0 = shape[0]

        def cast(self, lo=0, hi=None, after=()):
            npieces = (self.n0 + self.step - 1) // self.step
            hi = npieces if hi is None else hi
            for pi_ in range(lo, hi):
                a = pi_ * self.step
                b_ = min(self.n0, a + self.step)
                sc.dma(pool, self.scr[a:b_], self.src[a:b_], writes=[buf("scr_%s_%d" % (self.name, pi_))], after=after)

        def get(self, idx):
            return self.scr[idx], buf("scr_%s_%d" % (self.name, idx // self.step))

    S_win = Scr("win", win_d, [27, 128, 8, 128], 3)
    S_wco = Scr("wco", wco_d, [8, 128, 4, 128], 4)
    S_wao = Scr("wao", wao_d, [8, 64, NH, 128], 4)
    S_wout = Scr("wout", wout_d, [8, 128, D], 2)
    S_wg = Scr("wg", wg_d, [NFF, 128, 8, 128], 2)
    S_wu = Scr("wu", wu_d, [NFF, 128, 8, 128], 2)
    S_wd = Scr("wd", wd_d, [NFF, 128, D], 2)
    WIN_CH = {c0: i_ for i_, c0 in enumerate(WIN_COLS)}

    def checkpoint(name):
        if dbg and dbg.get("stop") == name:
            raise StopBuild()

    out_tickets = []
    try:
        _emit_all = True
    except Exception:
        pass
    def emit_all():
        dbg_st = {"off": 0}

        def dump(name, ap_fn, bufs, rows=128, cols=GT):
            if not dbg or name not in dbg["want"]:
                return
            off = dbg_st["off"]
            dbg_st["off"] += cols
            dbg.setdefault("layout", {})[name] = (off, rows, cols)
            sc.op(dve, lambda e: e.tensor_copy(out=dbgt[0:rows, 0:cols], in_=ap_fn()), reads=bufs, writes=[buf("dbgt")])
            sc.dma(sp, dbg_d[0:rows, off:off + cols], dbgt[0:rows, 0:cols], reads=[buf("dbgt")], writes=[buf("dbg_out")])

        st = {"w": 0, "wd": 0, "ps": 0}
        def dump_setup():
            dump("modT", lambda: modT[:, :], [buf("modT")], cols=48)
            dump("gate", lambda: gate_bc[:, 0:512], [buf("gate_bc")])
            dump("gatef", lambda: gate_bc[:, D:D + 512], [buf("gate_bc")])
            dump("cos", lambda: cosT[64:96, 0:512], [buf("cs")], rows=32)
            dump("sin", lambda: sinT[64:96, 1536:2048], [buf("cs")], rows=32)

        def barrier():
            tks = [(e_, sc.cnt[e_]) for e_ in ("pe", "act", "dve", "pool") if sc.cnt[e_] > 0]
            for e_ in ("pe", "act", "dve", "pool", "sp"):
                sc.wait_all(e_, tks)

        sc.dma(sp, vecs[:, :], vecs_d, writes=[buf("vecs")])
        sc.dma(pool, ident[:, :], ident_d, writes=[buf("ident")])
        for i_ in range(2):
            sc.dma(pool, wkv[:, i_ * 512:(i_ + 1) * 512], wkv_d[:, i_ * 512:(i_ + 1) * 512], writes=[buf("wkv")])
        sc.dma(pool, wv[:, :, :], wkv_d.rearrange("p (h e) -> p h e", h=NH)[:, :, 64:128], writes=[buf("wv")])
        sc.dma(pool, wkrot[:, :, :], wkrot_d.rearrange("(k p) n -> p k n", p=128), writes=[buf("wkrot")])
        sc.dma(pool, wkr[:, :, :], wkr_d.rearrange("(k p) n -> p k n", p=128), writes=[buf("wkr")])
        sc.op(pool, lambda e: e.tensor_scalar(out=wkrot[:, :, 0:16], in0=wkrot[:, :, 0:16], scalar1=-1.0, scalar2=None, op0=ALU.mult),
              reads=[buf("wkrot")], writes=[buf("wkrot")])
        sc.op(dve, lambda e: e.memset(ones_f[:, :], 1.0), writes=[buf("ones_f")])
        sc.op(dve, lambda e: e.memset(epsc[:, 0:1], EPS_RMS), writes=[buf("epsc")])
        sc.op(dve, lambda e: e.memset(epsc[:, 1:2], EPS_LN), writes=[buf("epsc")])
        for i, val in enumerate((1.0 / 128, 1.0 / 256, 1.0 / 512, 1.0)):
            sc.op(dve, lambda e, i=i, val=val: e.memset(ones_bf[:, i, :], val), writes=[buf("ones_bf")])
        sc.op(act, lambda e: e.activation(out=caT[:, :], in_=vecs[:, V_C:V_C + 8], func=AF.Silu),
              reads=[buf("vecs")], writes=[buf("caT")])

        wada_v = wada_d.rearrange("(k p) n -> p k n", p=128)

        def mod_chunks(j_lo, j_hi, trig=(), evac=True):
            for j in range(j_lo, j_hi):
                i = st["w"] % NW
                st["w"] += 1
                w = wbuf[i]
                wb_ = buf("wbuf%d" % i)
                sc.dma(pool, w[:, :, :], wada_v[:, :, j * 128:(j + 1) * 128], writes=[wb_], after=trig)
                for k in range(8):
                    sc.op(pe, lambda e, w=w, k=k, j=j: e.matmul(PSB[2][:, j:j + 1], w[:, k, :], caT[:, k:k + 1], start=(k == 0), stop=(k == 7)),
                          reads=[buf("caT"), wb_], writes=[PB[4]], ms=(k == 7))
            if evac:
                mod_evac(j_lo, j_hi)

        def mod_evac(j_lo, j_hi):
            sc.op(dve, lambda e: e.tensor_tensor(out=modT[:, j_lo:j_hi], in0=PSB[2][:, j_lo:j_hi], in1=vecs[:, V_BADA + j_lo:V_BADA + j_hi],
                                                 op=ALU.add),
                  reads=[PB[4], buf("vecs")], writes=[buf("modT")])

        def mod_gs(half):
            j0 = half * 24
            gcol = V_GMIX if half == 0 else V_GFFN
            sc.op(dve, lambda e: e.scalar_tensor_tensor(out=gsT[:, half * 8:half * 8 + 8], in0=modT[:, j0 + 8:j0 + 16], scalar=1.0,
                                                        in1=vecs[:, gcol:gcol + 8], op0=ALU.add, op1=ALU.mult),
                  reads=[buf("modT"), buf("vecs")], writes=[buf("gsT")])

        def bcast_cols(src_fn, src_bufs, dst, doff):
            for c in range(8):
                pb = 5 + c // 4
                sc.op(dve, lambda e, c=c: e.tensor_scalar(out=dgt[:, c % 4, :], in0=ident[:, :], scalar1=src_fn(c), scalar2=None, op0=ALU.mult),
                      reads=src_bufs + [buf("ident")], writes=[buf("dgt%d" % (c % 4))])
                sc.op(pe, lambda e, c=c, pb=pb: e.matmul(bank(pb)[:, (c % 4) * 128:(c % 4 + 1) * 128], ones_f[:, :], dgt[:, c % 4, :], start=True, stop=True),
                      reads=[buf("ones_f"), buf("dgt%d" % (c % 4))], writes=[PB[pb]])
            for i in range(2):
                sc.op(dve, lambda e, i=i: e.tensor_copy(out=dst[:, doff + i * 512:doff + (i + 1) * 512], in_=bank(5 + i)),
                      reads=[PB[5 + i]], writes=[buf("gate_bc")])

        sc.op(dve, lambda e: e.memset(vaug[:, :, :, :], 0.0), writes=[buf("vaug")])
        sc.op(dve, lambda e: e.memset(vaug[:, :, :, 64:65], 1.0), writes=[buf("vaug")])
        sc.op(dve, lambda e: e.memset(zT[:, :, :], 0.0), writes=[buf("zTb%d" % b_) for b_ in range(5)])
        mod_chunks(0, 16, evac=False)
        S_win.cast(0, 3)
        R = slice(64, 96)
        rbuf = [buf("rop")]
        for cg in range(2):
            csl = slice(cg * 1024, (cg + 1) * 1024)
            sc.dma(sp, ropi[R, 0, :], bass.AP(pos_h, cg * 1024, [[0, 32], [1, 1024]]), writes=rbuf)
            sc.op(dve, lambda e: e.tensor_copy(out=rop[R, 1, :], in_=ropi[R, 0, :]), reads=rbuf, writes=rbuf)
            sc.op(dve, lambda e: e.tensor_scalar(out=rop[R, 1, :], in0=rop[R, 1, :], scalar1=vecs[R, V_INVF:V_INVF + 1], scalar2=None,
                                                 op0=ALU.mult), reads=rbuf + [buf("vecs")], writes=rbuf)
            for which, dst in ((0, sinT), (1, cosT)):
                si = 1
                if which == 1:
                    sc.op(dve, lambda e: e.tensor_scalar(out=rop[R, 0, :], in0=rop[R, 1, :], scalar1=math.pi / 2, scalar2=None,
                                                         op0=ALU.add), reads=rbuf, writes=rbuf)
                    si = 0
                sc.op(dve, lambda e, si=si: e.tensor_scalar(out=rop[R, 2, :], in0=rop[R, si, :], scalar1=1.0 / TWO_PI, scalar2=None,
                                                           op0=ALU.mult), reads=rbuf, writes=rbuf)
                sc.op(dve, lambda e: e.tensor_copy(out=ropi[R, 3, :], in_=rop[R, 2, :]), reads=rbuf, writes=rbuf)
                sc.op(dve, lambda e: e.tensor_copy(out=rop[R, 2, :], in_=ropi[R, 3, :]), reads=rbuf, writes=rbuf)
                sc.op(dve, lambda e, si=si: e.scalar_tensor_tensor(out=rop[R, 3, :], in0=rop[R, 2, :], scalar=-RC1, in1=rop[R, si, :],
                                                                  op0=ALU.mult, op1=ALU.add), reads=rbuf, writes=rbuf)
                sc.op(dve, lambda e: e.scalar_tensor_tensor(out=rop[R, 3, :], in0=rop[R, 2, :], scalar=-RC2, in1=rop[R, 3, :],
                                                            op0=ALU.mult, op1=ALU.add), reads=rbuf, writes=rbuf)
                sc.op(dve, lambda e: e.tensor_scalar(out=rop[R, 3, :], in0=rop[R, 3, :], scalar1=-PI_CL, scalar2=PI_CL,
                                                     op0=ALU.max, op1=ALU.min), reads=rbuf, writes=rbuf)
                sc.op(act, lambda e, dst=dst, csl=csl: e.activation(out=dst[R, csl], in_=rop[R, 3, :], func=AF.Sin),
                      reads=rbuf, writes=[buf("cs")])

        for c in range(4):
            id_b = bass.AP(ident, 0, [[128, 128], [0, 31], [1, 128]])
            w_b = bass.AP(vecs, V_WDW + c * 31, [[NVEC, 128], [1, 31], [0, 128]])
            sc.op(dve, lambda e, c=c, id_b=id_b, w_b=w_b: e.tensor_tensor(out=dg[:, c, :, :], in0=id_b, in1=w_b, op=ALU.mult),
                  reads=[buf("ident"), buf("vecs")], writes=[buf("dg%d" % c)])

        mod_evac(0, 16)
        mod_gs(0)

        def mod_late(part, trig=()):
            if part == 0:
                mod_chunks(16, 24, trig)
                bcast_cols(lambda c: modT[:, 16 + c:17 + c], [buf("modT")], gate_bc, 0)
            mod_chunks(24 + part * 6, 30 + part * 6, trig)
            if part == 3:
                mod_gs(1)
                bcast_cols(lambda c: modT[:, 40 + c:41 + c], [buf("modT")], gate_bc, D)
                bcast_cols(lambda c: vecs[:, V_GFIN + c:V_GFIN + c + 1], [buf("vecs")], gfin_bc, 0)
                dump_setup()

        checkpoint('setup')


        def load_w(W_view, col0, ncols, nk):
            i = st["w"] % NW
            st["w"] += 1
            scr_, idx_fn = W_view
            ap_, sbuf_ = scr_.get(idx_fn(col0))
            sc.dma(sp, wbuf[i][:, 0:nk, :], ap_, reads=[sbuf_], writes=[buf("wbuf%d" % i)])
            return wbuf[i], buf("wbuf%d" % i)

        def next_bank(lst):
            b = lst[st["ps"] % len(lst)]
            st["ps"] += 1
            return b

        win_v = (S_win, lambda col0: WIN_CH[col0])
        wg_v = (S_wg, lambda col0: col0 // 128)
        wu_v = (S_wu, lambda col0: col0 // 128)
        wco_v = (S_wco, lambda col0: col0 // 128)

        def lin_T(pb, W_view, col0, ncols, nk, rhs_fn, rhs_bufs, out_rows=None, tile_pos=None, wt=None):
            if wt is None:
                w, wb_ = load_w(W_view, col0, ncols, nk)
                lhs = lambda k: w[:, k, 0:ncols]
            else:
                lhs, wb_ = wt
            o = bank(pb) if out_rows is None else bank_rows(pb, out_rows[0], out_rows[1])
            if out_rows is None and ncols < 128:
                o = bank_rows(pb, 0, ncols)
            for k in range(nk):
                kw = {}
                if tile_pos is not None:
                    kw["tile_position"] = tile_pos
                sc.op(pe, lambda e, o=o, k=k, lhs=lhs, kw=kw: e.matmul(o, lhs(k), rhs_fn(k), start=(k == 0), stop=(k == nk - 1), **kw),
                      reads=[wb_] + rhs_bufs, writes=[PB[pb]], ms=(k == nk - 1))

        def load_x(g, xb=None, pre="xg"):
            xb = xg if xb is None else xb
            for j in range(4):
                t = g * 4 + j
                sc.dma(pool, xb[:, j, :], x_d[t * 128:(t + 1) * 128, :], writes=[buf("%s%d" % (pre, j))])

        def norm_A(xb, pre, tiles=(0, 1, 2, 3)):
            for j in tiles:
                xb_j = buf("%s%d" % (pre, j))
                sc.op(act, lambda e, j=j: e.activation(out=junk[:, :], in_=xb[:, j, :], func=AF.Square, accum_out=ss[:, j:j + 1]),
                      reads=[xb_j], writes=[buf("sq"), buf("ss%d" % j)])
                sc.op(act, lambda e, j=j: e.activation(out=sstmp[:, j:j + 1], in_=ss[:, j:j + 1], func=AF.Ln, scale=1.0 / D, bias=epsc[:, 0:1]),
                      reads=[buf("ss%d" % j), buf("epsc")], writes=[buf("sstmp%d" % j)])
                sc.op(act, lambda e, j=j: e.activation(out=rstd[:, j:j + 1], in_=sstmp[:, j:j + 1], func=AF.Exp, scale=-0.5),
                      reads=[buf("sstmp%d" % j)], writes=[buf("rstd%d" % j)])
                sc.op(act, lambda e, j=j: e.activation(out=xn[:, j, :], in_=xb[:, j, :], func=AF.Identity, scale=rstd[:, j:j + 1]),
                      reads=[xb_j, buf("rstd%d" % j)], writes=[buf("xn%d" % j)])

        def norm_B(gs_off, sh_off):
            for j in range(4):
                for c in range(8):
                    sc.op(pe, lambda e, c=c, j=j: e.transpose(PSBF[c // 4][:, (c % 4) * 512 + j * 128:(c % 4) * 512 + (j + 1) * 128],
                                                             xn[:, j, c * 128:(c + 1) * 128], ident[:, :]),
                          reads=[buf("xn%d" % j), buf("ident")], writes=[PB[c // 2]], ms=(j == 3))
            for c in range(8):
                src = PSBF[c // 4][:, (c % 4) * 512:(c % 4 + 1) * 512]
                if (c // 2) % 2 == 0:
                    sc.op(dve, lambda e, c=c, src=src: e.tensor_scalar(out=hT[:, c, :], in0=src, scalar1=gsT[:, gs_off + c:gs_off + c + 1],
                                                                      scalar2=modT[:, sh_off + c:sh_off + c + 1], op0=ALU.mult, op1=ALU.add),
                          reads=[PB[c // 2], buf("gsT"), buf("modT")], writes=[buf("hT%d" % c)])
                else:
                    sc.op(act, lambda e, c=c, src=src: e.activation(out=hT[:, c, :], in_=src, func=AF.Identity,
                                                                   scale=gsT[:, gs_off + c:gs_off + c + 1],
                                                                   bias=modT[:, sh_off + c:sh_off + c + 1]),
                          reads=[PB[c // 2], buf("gsT"), buf("modT")], writes=[buf("hT%d" % c)])

        def norm_group(g, gs_off, sh_off, first):
            norm_A(xg, "xg")
            norm_B(gs_off, sh_off)

        def rms_feat(src_banks, nchunk, ones_idx, gcol, dst_fn, dst_buf):
            for i, pb in enumerate(src_banks):
                sc.op(act, lambda e, i=i, pb=pb: e.activation(out=sq[:, i, :], in_=bank(pb), func=AF.Square),
                      reads=[PB[pb]], writes=[buf("sq")])
            sb_ = 7
            for i in range(nchunk):
                sc.op(pe, lambda e, i=i: e.matmul(bank(sb_), ones_bf[:, ones_idx, :], sq[:, i, :], start=(i == 0), stop=(i == nchunk - 1)),
                      reads=[buf("sq"), buf("ones_bf")], writes=[PB[sb_]], ms=(i == nchunk - 1))
            sc.op(act, lambda e: e.activation(out=nrm_tmp[:, :], in_=bank(sb_), func=AF.Ln, bias=epsc[:, 0:1]),
                  reads=[PB[sb_], buf("epsc")], writes=[buf("nrm_tmp")])
            sc.op(act, lambda e: e.activation(out=nrm_rs[:, :], in_=nrm_tmp[:, :], func=AF.Exp, scale=-0.5),
                  reads=[buf("nrm_tmp")], writes=[buf("nrm_rs")])
            for i, pb in enumerate(src_banks):
                sc.op(dve, lambda e, i=i, pb=pb: e.scalar_tensor_tensor(out=dst_fn(i), in0=bank(pb), scalar=vecs[:, gcol + i:gcol + i + 1],
                                                                       in1=nrm_rs[:, :], op0=ALU.mult, op1=ALU.mult),
                      reads=[PB[pb], buf("vecs"), buf("nrm_rs")], writes=[dst_buf])

        def rope_evac(pbA, pbB, g, dst_ap, dst_buf):
            gs_ = slice(g * GT, (g + 1) * GT)
            sc.op(dve, lambda e: e.tensor_tensor(out=rp1[R, :], in0=bank_rows(pbA, 64, 96), in1=cosT[R, gs_], op=ALU.mult),
                  reads=[PB[pbA], buf("cs")], writes=[buf("nrm_tmp")])
            sc.op(dve, lambda e: e.tensor_tensor(out=rp2[R, :], in0=bank_rows(pbB, 64, 96), in1=sinT[R, gs_], op=ALU.mult),
                  reads=[PB[pbB], buf("cs")], writes=[buf("nrm_rs")])
            sc.op(dve, lambda e: e.tensor_tensor(out=dst_ap, in0=rp1[R, :], in1=rp2[R, :], op=ALU.add),
                  reads=[buf("nrm_tmp"), buf("nrm_rs")], writes=[dst_buf])

        hbufs = [buf("hT%d" % c_) for c_ in range(8)]
        def C_group(g):
            gsl = slice(g * GT, (g + 1) * GT)
            if g == 0:
                dump("hT0", lambda: hT[:, 0, :], [buf("hT0")])
                dump("hT7", lambda: hT[:, 7, :], [buf("hT7")])
            rhs_h = lambda k: hT[:, k, :]
            lin_T(4, win_v, C_KVA, 128, 8, rhs_h, hbufs)
            lin_T(5, None, 0, 32, 8, rhs_h, hbufs, out_rows=(64, 96), tile_pos=(0, 64),
                  wt=(lambda k: wkr[:, k, :], buf("wkr")))
            lin_T(6, None, 0, 32, 8, rhs_h, hbufs, out_rows=(64, 96), tile_pos=(0, 64),
                  wt=(lambda k: wkrot[:, k, :], buf("wkrot")))
            def glu(c):
                pu = next_bank([0, 1, 2, 3])
                lin_T(pu, win_v, C_U + c * 128, 128, 8, rhs_h, hbufs)
                pg = next_bank([0, 1, 2, 3])
                lin_T(pg, win_v, C_UG + c * 128, 128, 8, rhs_h, hbufs)
                sc.op(act, lambda e, pg=pg, c=c: e.activation(out=sig[:, c % 2, :], in_=bank(pg), func=AF.Sigmoid),
                      reads=[PB[pg]], writes=[buf("sig%d" % (c % 2))])
                sc.op(dve, lambda e, pu=pu, c=c, g=g: e.tensor_tensor(out=zT[:, c, 16 + g * GT:16 + (g + 1) * GT], in0=bank(pu),
                                                                     in1=sig[:, c % 2, :], op=ALU.mult),
                      reads=[PB[pu], buf("sig%d" % (c % 2))], writes=[buf("zTb%d" % g), buf("zTb%d" % (g + 1))])

            glu(0)
            glu(1)
            rms_feat([4], 1, 0, V_GKV, lambda i: kvnT[:, :], buf("kvnT"))
            rope_evac(5, 6, g, kT[R, 0, gsl], buf("kTr0"))
            for h in range(1, NH):
                sc.op((act if h % 2 else dve), lambda e, h=h, gsl=gsl: (e.activation(out=kT[R, h, gsl], in_=kT[R, 0, gsl], func=AF.Identity) if h % 2 else e.tensor_copy(out=kT[R, h, gsl], in_=kT[R, 0, gsl])), reads=[buf("kTr0")], writes=[buf("kTr%d" % h)])
            if g == 0:
                dump("kvnT", lambda: kvnT[:, :], [buf("kvnT")])
                dump("krope", lambda: kT[R, 0, 0:GT], [buf("kTr0")], rows=32)
            for h in range(NH):
                pb = next_bank([0, 1, 2, 3])
                sc.op(pe, lambda e, h=h, pb=pb: e.matmul(bank_rows(pb, 0, 64), wkv[:, h * 128:h * 128 + 64], kvnT[:, :], start=True, stop=True),
                      reads=[buf("wkv"), buf("kvnT")], writes=[PB[pb]])
                sc.op(act, lambda e, h=h, pb=pb, gsl=gsl: e.activation(out=kT[0:64, h, gsl], in_=bank_rows(pb, 0, 64), func=AF.Identity),
                      reads=[PB[pb]], writes=[buf("kTn%d" % h)])
            for j in range(4):
                t = g * 4 + j
                pb = next_bank([0, 1, 2, 3])
                sc.op(pe, lambda e, j=j, pb=pb: e.matmul(bank(pb), kvnT[:, j * 128:(j + 1) * 128], wv[:, :, :].rearrange("p h e -> p (h e)"),
                                                        start=True, stop=True),
                      reads=[buf("wv"), buf("kvnT")], writes=[PB[pb]])
                sc.op(dve, lambda e, t=t, pb=pb: e.tensor_copy(out=vaug[:, t, :, 0:64], in_=bank(pb).rearrange("p (h e) -> p h e", h=NH)),
                      reads=[PB[pb]], writes=[buf("vaug")])
            glu(2)
            glu(3)
            if g < NG - 1:
                load_x(g + 1)
            else:
                load_x(0)
            trig = [buf("zTb%d" % g).w]
            if g == 0:
                S_win.cast(3, None, after=trig)
            elif g == 1:
                mod_late(0, trig)
                S_wco.cast(after=trig)
                S_wao.cast(after=trig)
                S_wout.cast(after=trig)
            elif g == 2:
                mod_late(2, trig)
                S_wg.cast(after=trig)
            else:
                S_wu.cast(after=trig)
                S_wd.cast(after=trig)
            if g == 0:
                dump("zT0", lambda: zT[:, 0, 16:16 + GT], [buf("zTb0"), buf("zTb1")])
                dump("kn0", lambda: kT[0:64, 0, 0:GT], [buf("kTn0")], rows=64)
                dump("v0", lambda: vaug[:, 0, 0:4, :].rearrange("p h e -> p (h e)"), [buf("vaug")], cols=4 * 66)


        def C2_group(g):
            gsl = slice(g * GT, (g + 1) * GT)
            for c in range(4):
                pb = c
                for k in range(31):
                    sc.op(pe, lambda e, c=c, k=k, pb=pb, g=g: e.matmul(bank(pb), dg[:, c, k, :], zT[:, c, g * GT + k + 1:g * GT + k + 1 + GT],
                                                                      start=(k == 0), stop=(k == 30)),
                          reads=[buf("dg%d" % c), buf("zTb%d" % g), buf("zTb%d" % (g + 1))], writes=[PB[pb]], ms=(k == 30))
                sc.op(act, lambda e, c=c, pb=pb: e.activation(out=zc[:, c, :], in_=bank(pb), func=AF.Identity,
                                                             bias=vecs[:, V_BDW + c:V_BDW + c + 1]),
                      reads=[PB[pb], buf("vecs")], writes=[buf("zc")])
                sc.op(dve, lambda e, c=c: e.tensor_copy(out=zcb[:, c, :], in_=zc[:, c, :]), reads=[buf("zc")], writes=[buf("zcb")])
                sc.op(act, lambda e, c=c: e.activation(out=sq[:, c, :], in_=zc[:, c, :], func=AF.Square), reads=[buf("zc")], writes=[buf("sq")])
            for c in range(4):
                sc.op(pe, lambda e, c=c: e.matmul(bank(4), ones_bf[:, 2, :], zcb[:, c, :], start=(c == 0), stop=(c == 3)),
                      reads=[buf("zcb"), buf("ones_bf")], writes=[PB[4]], ms=(c == 3))
            for c in range(4):
                sc.op(pe, lambda e, c=c: e.matmul(bank(5), ones_bf[:, 2, :], sq[:, c, :], start=(c == 0), stop=(c == 3)),
                      reads=[buf("sq"), buf("ones_bf")], writes=[PB[5]], ms=(c == 3))
            sc.op(act, lambda e: e.activation(out=mean_sb[:, :], in_=bank(4), func=AF.Identity), reads=[PB[4]], writes=[buf("mean_sb")])
            sc.op(dve, lambda e: e.tensor_tensor(out=m2[:, :], in0=mean_sb[:, :], in1=mean_sb[:, :], op=ALU.mult),
                  reads=[buf("mean_sb")], writes=[buf("m2")])
            sc.op(dve, lambda e: e.scalar_tensor_tensor(out=nrm_tmp[:, :], in0=bank(5), scalar=EPS_LN, in1=m2[:, :], op0=ALU.add, op1=ALU.subtract),
                  reads=[PB[5], buf("m2")], writes=[buf("nrm_tmp")])
            sc.op(act, lambda e: e.activation(out=nrm_tmp[:, :], in_=nrm_tmp[:, :], func=AF.Ln),
                  reads=[buf("nrm_tmp")], writes=[buf("nrm_tmp")])
            sc.op(act, lambda e: e.activation(out=nrm_rs[:, :], in_=nrm_tmp[:, :], func=AF.Exp, scale=-0.5),
                  reads=[buf("nrm_tmp")], writes=[buf("nrm_rs")])
            for c in range(4):
                sc.op(dve, lambda e, c=c: e.tensor_tensor(out=zc[:, c, :], in0=zc[:, c, :], in1=mean_sb[:, :], op=ALU.subtract),
                      reads=[buf("zc"), buf("mean_sb")], writes=[buf("zc")])
                sc.op(dve, lambda e, c=c: e.scalar_tensor_tensor(out=zc[:, c, :], in0=zc[:, c, :], scalar=vecs[:, V_GLN + c:V_GLN + c + 1],
                                                                in1=nrm_rs[:, :], op0=ALU.mult, op1=ALU.mult),
                      reads=[buf("zc"), buf("nrm_rs"), buf("vecs")], writes=[buf("zc")])
                sc.op(act, lambda e, c=c, gsl=gsl: e.activation(out=zsT[:, c, gsl], in_=zc[:, c, :], func=AF.Silu,
                                                               bias=vecs[:, V_BLN + c:V_BLN + c + 1]),
                      reads=[buf("zc"), buf("vecs")], writes=[buf("zTb%d" % g)])
            if g == 0:
                dump("zs0", lambda: zsT[:, 0, 0:GT], [buf("zTb0")])

        rhs_hq = lambda k: hT[:, k, :]
        rhs_q = lambda k: qanT[:, k, :]

        def q_pre_A():
            lin_T(4, win_v, C_QA, 128, 8, rhs_hq, hbufs)
            lin_T(5, win_v, C_QA + 128, 128, 8, rhs_hq, hbufs)
            rms_feat([4, 5], 2, 1, V_GQA, lambda i: qanT[:, i, :], buf("qanT"))

        load_x(0)
        norm_A(xg, "xg")
        norm_B(0, 0)
        C_group(0)
        norm_A(xg, "xg")
        norm_B(0, 0)
        C_group(1)
        norm_A(xg, "xg")
        barrier()
        C2_group(0)
        mod_late(1)
        norm_B(0, 0)
        C_group(2)
        norm_A(xg, "xg")
        C2_group(1)
        mod_late(3)
        norm_B(0, 0)
        C_group(3)
        norm_A(xg, "xg")
        C2_group(2)
        norm_B(0, 0)
        q_pre_A()
        C2_group(3)
        checkpoint('C2')
        barrier()
        for i_ in range(2):
            sc.dma(pool, wq[:, :, i_ * 384:(i_ + 1) * 384], wq_d.rearrange("(k p) n -> p k n", p=128)[:, :, i_ * 384:(i_ + 1) * 384], writes=[buf("wq")])
        sc.dma(pool, wqrot, wqrot_d.rearrange("(k p) n -> p k n", p=128), writes=[buf("wqrot")])
        wqr4 = wqrot.rearrange("p k (h e) -> p k h e", h=NH)
        sc.op(pool, lambda e: e.tensor_scalar(out=wqr4[:, :, :, 0:16], in0=wqr4[:, :, :, 0:16], scalar1=-1.0, scalar2=None, op0=ALU.mult),
              reads=[buf("wqrot")], writes=[buf("wqrot")])
        def q_head(h, gg):
            pa, pb2 = 6, 7
            lin_T(pa, None, 0, 96, 2, rhs_q, [buf("qanT")], wt=(lambda k, h=h: wq[:, k, h * 96:(h + 1) * 96], buf("wq")))
            lin_T(pb2, None, 0, 32, 2, rhs_q, [buf("qanT")], out_rows=(64, 96), tile_pos=(0, 64),
                  wt=(lambda k, h=h: wqrot[:, k, h * 32:(h + 1) * 32], buf("wqrot")))
            sc.op(dve, lambda e, h=h, pa=pa: e.tensor_copy(out=qT[0:64, h, :], in_=bank_rows(pa, 0, 64)),
                  reads=[PB[pa]], writes=[buf("qT%d" % h)])
            rope_evac(pa, pb2, gg, qT[R, h, :], buf("qT%d" % h))

        XB = [(xg, "xg"), (xgB, "xh")]
        for g in range(NG):
            gsl = slice(g * GT, (g + 1) * GT)
            xb, pre = XB[g % 2]
            xb_n, pre_n = XB[(g + 1) % 2]
            if g == 0:
                pass
            rhs_h = lambda k: hT[:, k, :]
            if g == 0:
                q_head(0, 0)
            if g == 0:
                dump("qT0", lambda: qT[0:96, 0, :], [buf("qT0")], rows=96)
            checkpoint('Dq')
            def S_pair(h, kp):
                pbs = (0, 1) if kp % 2 == 0 else (2, 3)
                for i in range(2):
                    kt = kp * 2 + i
                    sc.op(pe, lambda e, h=h, kt=kt, pb=pbs[i]: e.matmul(bank(pb), kT[0:96, h, kt * 128:(kt + 1) * 128], qT[0:96, h, :],
                                                                       start=True, stop=True),
                          reads=[buf("kTr%d" % h), buf("kTn%d" % h), buf("qT%d" % h)], writes=[PB[pbs[i]]], ms=(i == 1))

            def E_pair(h, kp):
                pbs = (0, 1) if kp % 2 == 0 else (2, 3)
                pi_ = (h * 8 + kp) % NPT
                pt = pT[pi_]
                sc.op(act, lambda e, pt=pt, j=pbs[0] // 2: e.activation(out=pt.rearrange("p a n -> p (a n)"), in_=PSB[j][:, :],
                                                                       func=AF.Exp, scale=ATTN_SCALE),
                      reads=[PB[pbs[0]], PB[pbs[1]]], writes=[buf("pT%d" % pi_)])

            def P_pair(h, kp):
                po = 4 + (h % 2)
                pi_ = (h * 8 + kp) % NPT
                pt = pT[pi_]
                ptb = buf("pT%d" % pi_)
                for i in range(2):
                    kt = kp * 2 + i
                    sc.op(pe, lambda e, h=h, kt=kt, i=i, pt=pt, po=po: e.matmul(bank_rows(po, 0, 66), vaug[:, kt, h, :], pt[:, i, :],
                                                                               start=(kt == 0), stop=(kt == 15)),
                          reads=[buf("vaug"), ptb], writes=[PB[po]], ms=(i == 1))

            def tail_a(h):
                po = 4 + (h % 2)
                sc.op(act, lambda e, po=po: e.activation(out=rden[64:65, :], in_=bank_rows(po, 64, 65), func=AF.Ln), reads=[PB[po]], writes=[buf("evt")])
                sc.op(act, lambda e: e.activation(out=rden[64:65, :], in_=rden[64:65, :], func=AF.Exp, scale=-1.0), reads=[buf("evt")], writes=[buf("evt")])

            def tail_b(h):
                po = 4 + (h % 2)
                pr = 6 + (h % 2)
                sc.op(pe, lambda e, pr=pr: e.matmul(bank_rows(pr, 0, 64), ones_f[64:65, 0:64], rden[64:65, :], start=True, stop=True),
                      reads=[buf("evt"), buf("ones_f")], writes=[PB[pr]])
                sc.op(dve, lambda e, pr=pr: e.tensor_copy(out=rb[:, :], in_=bank_rows(pr, 0, 64)), reads=[PB[pr]], writes=[buf("nrm_tmp")])
                sc.op(dve, lambda e, h=h, po=po: e.tensor_tensor(out=aoT[:, h, :], in0=bank_rows(po, 0, 64), in1=rb[:, :], op=ALU.mult),
                      reads=[PB[po], buf("nrm_tmp")], writes=[buf("aoT")])

            S_pair(0, 0)
            pending = None
            for it in range(NH * 8):
                h, kp = divmod(it, 8)
                if it + 1 < NH * 8:
                    S_pair(*divmod(it + 1, 8))
                E_pair(h, kp)
                if it >= 1:
                    hp, kpp = divmod(it - 1, 8)
                    P_pair(hp, kpp)
                    if kpp == 7:
                        pending = hp
                if kp == 1 and pending is not None:
                    tail_a(pending)
                if kp == 2 and pending is not None:
                    tail_b(pending)
                    pending = None
                if kp == 4 and h + 1 < NH:
                    q_head(h + 1, g)
            P_pair(NH - 1, 7)
            tail_a(NH - 1)
            tail_b(NH - 1)
            if g == 0:
                dump("ao0", lambda: aoT[:, 0, :], [buf("aoT")], rows=64)
            checkpoint('Dattn')
            if g + 1 < NG:
                load_x(g + 1, xb_n, pre_n)
            for c in range(8):
                pya, pga, pgb, pyb = (0, 1, 2, 3) if c % 2 == 0 else (4, 5, 6, 7)
                iw = st["w"] % NW
                st["w"] += 1
                ap_, sbuf_ = S_wao.get(c)
                sc.dma(sp, wbuf[iw][0:64, :, :], ap_, reads=[sbuf_], writes=[buf("wbuf%d" % iw)])
                for h in range(NH):
                    sc.op(pe, lambda e, h=h, iw=iw, pya=pya: e.matmul(bank(pya), wbuf[iw][0:64, h, :], aoT[:, h, :],
                                                                     start=(h == 0), stop=(h == NH - 1)),
                          reads=[buf("wbuf%d" % iw), buf("aoT")], writes=[PB[pya]], ms=(h == NH - 1))
                lin_T(pga, win_v, C_GA + c * 128, 128, 8, rhs_h, hbufs)
                lin_T(pgb, win_v, C_GB + c * 128, 128, 8, rhs_h, hbufs)
                lin_T(pyb, wco_v, c * 128, 128, 4, lambda k, gsl=gsl: zsT[:, k, gsl], [buf("zTb%d" % g)])
                sc.op(act, lambda e, pga=pga: e.activation(out=sig[:, 0, :], in_=bank(pga), func=AF.Sigmoid), reads=[PB[pga]], writes=[buf("sig0")])
                sc.op(act, lambda e, pgb=pgb: e.activation(out=sig[:, 1, :], in_=bank(pgb), func=AF.Sigmoid), reads=[PB[pgb]], writes=[buf("sig1")])
                sc.op(dve, lambda e, pya=pya: e.tensor_tensor(out=sig[:, 0, :], in0=bank(pya), in1=sig[:, 0, :], op=ALU.mult),
                      reads=[PB[pya], buf("sig0")], writes=[buf("sig0")])
                sc.op(dve, lambda e, pyb=pyb: e.tensor_tensor(out=sig[:, 1, :], in0=bank(pyb), in1=sig[:, 1, :], op=ALU.mult),
                      reads=[PB[pyb], buf("sig1")], writes=[buf("sig1")])
                sc.op(dve, lambda e, c=c: e.tensor_tensor(out=mT[:, c, :], in0=sig[:, 0, :], in1=sig[:, 1, :], op=ALU.add),
                      reads=[buf("sig0"), buf("sig1")], writes=[buf("actT")])
            if g == 0:
                dump("mT0", lambda: mT[:, 0, :], [buf("actT")])
            checkpoint('Dmerge')
            def tok_proj(W_d, row0, nk, lhs_fn, lhs_bufs, gate_off, xb=xb, pre=pre, tile_groups=((0, 1, 2, 3),), hook=None, hf_outer=False):
                for tgi, tg in enumerate(tile_groups):
                    if tgi == 1 and hook is not None:
                        hook()
                    nt = len(tg)
                    bank_of = {}
                    for hf in range(2):
                        for jj, j in enumerate(tg):
                            bank_of[(hf, j)] = (hf * 4 + j) if nt == 4 else ((j // 2) * 4 + hf * 2 + jj)
                    for k in range(nk):
                        i = st["w"] % NW
                        st["w"] += 1
                        wfl = wbuf[i][:, :, :].rearrange("p k n -> p (k n)")
                        ap_, sbuf_ = W_d.get(row0 // 128 + k)
                        sc.dma(sp, wfl, ap_, reads=[sbuf_], writes=[buf("wbuf%d" % i)])
                        for hf in range(2):
                            for jj, j in enumerate(tg):
                                pb = bank_of[(hf, j)]
                                last = (hf == 1 and jj == nt - 1)
                                sc.op(pe, lambda e, wfl=wfl, k=k, j=j, hf=hf, pb=pb: e.matmul(bank(pb), lhs_fn(k, j), wfl[:, hf * 512:(hf + 1) * 512],
                                                                                            start=(k == 0), stop=(k == nk - 1)),
                                      reads=[buf("wbuf%d" % i)] + lhs_bufs, writes=[PB[pb]], ms=(k == nk - 1 or last))
                    order = [(j, hf) for hf in range(2) for j in tg] if hf_outer else [(j, hf) for j in tg for hf in range(2)]
                    for ei, (j, hf) in enumerate(order):
                        if True:
                            pb = bank_of[(hf, j)]
                            ev_, evb_ = (evt, buf("evt")) if ei % 2 == 0 else (nrm_rs, buf("nrm_rs"))
                            sc.op(dve, lambda e, pb=pb, hf=hf, ev_=ev_: e.tensor_tensor(out=ev_[:, :], in0=bank(pb),
                                                                                       in1=gate_bc[:, gate_off + hf * 512:gate_off + (hf + 1) * 512], op=ALU.mult),
                                  reads=[PB[pb], buf("gate_bc")], writes=[evb_])
                            sc.op(dve, lambda e, j=j, hf=hf, ev_=ev_: e.tensor_tensor(out=xb[:, j, hf * 512:(hf + 1) * 512], in0=xb[:, j, hf * 512:(hf + 1) * 512],
                                                                                     in1=ev_[:, :], op=ALU.add),
                                  reads=[buf("%s%d" % (pre, j)), evb_], writes=[buf("%s%d" % (pre, j))])

            tok_proj(S_wout, 0, 8, lambda k, j: mT[:, k, j * 128:(j + 1) * 128], [buf("actT")], 0, tile_groups=((0, 1), (2, 3)))
            if g == 0:
                dump("x1", lambda: xg[:, 0, 0:GT], [buf("xg0")])
            checkpoint('Dx1')
            norm_A(xb, pre)
            norm_B(8, 24)
            for part in range(2):
                for fi in range(NFH):
                    f = part * NFH + fi
                    if part == 0 and g + 1 < NG and fi in (2, 4, 6, 8):
                        norm_A(xb_n, pre_n, tiles=((fi - 2) // 2,))
                    pg_, pu_ = (0, 1) if f % 2 == 0 else (2, 3)
                    lin_T(pg_, wg_v, f * 128, 128, 8, rhs_h, hbufs)
                    lin_T(pu_, wu_v, f * 128, 128, 8, rhs_h, hbufs)
                    sc.op(act, lambda e, pg_=pg_, f=f: e.activation(out=sig[:, f % 2, :], in_=bank(pg_), func=AF.Silu),
                          reads=[PB[pg_]], writes=[buf("sig%d" % (f % 2))])
                    sc.op(dve, lambda e, pu_=pu_, f=f, fi=fi: e.tensor_tensor(out=actT[:, fi, :], in0=bank(pu_), in1=sig[:, f % 2, :], op=ALU.mult),
                          reads=[PB[pu_], buf("sig%d" % (f % 2))], writes=[buf("actT")])
                if part == 1 and g + 1 < NG:
                    norm_B(0, 0)
                    q_pre_A()
                    tok_proj(S_wd, part * NFH * 128, NFH, lambda k, j: actT[:, k, j * 128:(j + 1) * 128], [buf("actT")], D,
                             tile_groups=((0, 1), (2, 3)), hook=lambda g=g: q_head(0, g + 1))
                else:
                    tok_proj(S_wd, part * NFH * 128, NFH, lambda k, j: actT[:, k, j * 128:(j + 1) * 128], [buf("actT")], D, hf_outer=(part == 0))
            for j in range(4):
                t = g * 4 + j
                xb_j = buf("%s%d" % (pre, j))
                sc.op(act, lambda e, j=j, xb=xb: e.activation(out=junk[:, :], in_=xb[:, j, :], func=AF.Square, accum_out=ss[:, j:j + 1]),
                      reads=[xb_j], writes=[buf("sq"), buf("ss%d" % j)])
                sc.op(act, lambda e, j=j: e.activation(out=sstmp[:, j:j + 1], in_=ss[:, j:j + 1], func=AF.Ln, scale=1.0 / D, bias=epsc[:, 0:1]),
                      reads=[buf("ss%d" % j), buf("epsc")], writes=[buf("sstmp%d" % j)])
                sc.op(act, lambda e, j=j: e.activation(out=rstd[:, j:j + 1], in_=sstmp[:, j:j + 1], func=AF.Exp, scale=-0.5),
                      reads=[buf("sstmp%d" % j)], writes=[buf("rstd%d" % j)])
                sc.op(dve, lambda e, j=j, xb=xb: e.scalar_tensor_tensor(out=xb[:, j, :], in0=xb[:, j, :], scalar=rstd[:, j:j + 1], in1=gfin_bc[:, :],
                                                                       op0=ALU.mult, op1=ALU.mult),
                      reads=[xb_j, buf("rstd%d" % j), buf("gate_bc")], writes=[xb_j])
                out_tickets.append(sc.dma(pool, out_d[t * 128:(t + 1) * 128, :], xb[:, j, :], reads=[xb_j], writes=[buf("outd%d" % t)]))

    try:
        emit_all()
    except StopBuild:
        pass

    if dbg and buf("dbg_out").w is not None:
        out_tickets.append(buf("dbg_out").w)
        for k_, v_ in buf("dbg_out").r.items():
            out_tickets.append(v_)
    sc.wait_all(sp, out_tickets)

    with contextlib.ExitStack() as stack:
        sc.alloc_sems(stack)
        block = stack.enter_context(nc.Block())

        @block.tensor
        def _(e):
            for th in sc.q["pe"]:
                th(e)

        @block.scalar
        def _(e):
            for th in sc.q["act"]:
                th(e)

        @block.vector
        def _(e):
            for th in sc.q["dve"]:
                th(e)

        @block.gpsimd
        def _(e):
            for th in sc.q["pool"]:
                th(e)

        @block.sync
        def _(e):
            for th in sc.q["sp"]:
                th(e)
    return nc, sc


WIN_COLS = ([C_KVA] + [C_U + 128 * i for i in range(4)] + [C_UG + 128 * i for i in range(4)] + [C_QA, C_QA + 128]
            + [C_GA + 128 * i for i in range(8)] + [C_GB + 128 * i for i in range(8)])


def chunk_cols(w, col_starts):
    K = w.shape[0]
    out = np.empty((len(col_starts), 128, K // 128, 128), dtype=w.dtype)
    for i, c0 in enumerate(col_starts):
        out[i] = w[:, c0:c0 + 128].reshape(K // 128, 128, 128).transpose(1, 0, 2)
    return out


def host_inputs(b, inp):
    f = np.float32
    w_in = np.ascontiguousarray(inp["w_in"][0], dtype=f)
    w_q = np.ascontiguousarray(inp["w_q_up"][0], dtype=f)
    wq3 = w_q.reshape(256, NH, 96)
    w_q_rot = np.ascontiguousarray(np.concatenate([wq3[:, :, 80:96], wq3[:, :, 64:80]], axis=2).reshape(256, NH * 32))
    kr = w_in[:, C_KR:C_KR + 32]
    w_k_rot = np.ascontiguousarray(np.concatenate([kr[:, 16:32], kr[:, 0:16]], axis=1))

    def colv(v, n):
        return np.asarray(v, dtype=f).reshape(n, 128).T

    vecs = np.zeros((128, NVEC), dtype=f)
    vecs[:, V_GMIX:V_GMIX + 8] = colv(inp["g_norm_mix"][0], 8)
    vecs[:, V_GFFN:V_GFFN + 8] = colv(inp["g_norm_ffn"][0], 8)
    vecs[:, V_GQA:V_GQA + 2] = colv(inp["g_q_a"][0], 2)
    vecs[:, V_GKV:V_GKV + 1] = colv(inp["g_kv_a"][0], 1)
    vecs[:, V_BDW:V_BDW + 4] = colv(inp["b_dw"][0], 4)
    vecs[:, V_GLN:V_GLN + 4] = colv(inp["g_conv_ln"][0], 4)
    vecs[:, V_BLN:V_BLN + 4] = colv(inp["b_conv_ln"][0], 4)
    wdw = np.asarray(inp["w_dw"][0], dtype=f)
    vecs[:, V_WDW:V_WDW + 124] = wdw.reshape(31, 4, 128).transpose(2, 1, 0).reshape(128, 124)
    inv_freq = (10000.0 ** (-np.arange(0, 32, 2, dtype=np.float32) / np.float32(32))).astype(f)
    vecs[64:96, V_INVF] = np.concatenate([inv_freq, inv_freq])
    vecs[:, V_C:V_C + 8] = colv(inp["c"][b], 8)
    vecs[:, V_BADA:V_BADA + 48] = colv(inp["b_ada"][0], 48)
    vecs[:, V_GFIN:V_GFIN + 8] = colv(inp["g_final"], 8)
    return {
        "x": np.ascontiguousarray(inp["x"][b], dtype=f),
        "pos": np.ascontiguousarray(np.asarray(inp["positions"][b]).reshape(1, S).astype(np.int32)),
        "w_ada": np.ascontiguousarray(inp["w_ada"][0], dtype=f),
        "w_in_c": chunk_cols(w_in, WIN_COLS),
        "w_k_r": np.ascontiguousarray(w_in[:, C_KR:C_KR + 32]),
        "w_q_up": w_q,
        "w_q_rot": w_q_rot,
        "w_k_rot": w_k_rot,
        "w_kv_up": np.ascontiguousarray(inp["w_kv_up"][0], dtype=f),
        "w_attn_o_c": np.ascontiguousarray(np.asarray(inp["w_attn_o"][0], dtype=f).reshape(NH, 64, 8, 128).transpose(2, 1, 0, 3)),
        "w_conv_out_c": chunk_cols(np.asarray(inp["w_conv_out"][0], dtype=f), [c_ * 128 for c_ in range(8)]),
        "w_out": np.ascontiguousarray(np.asarray(inp["w_out"][0], dtype=f).reshape(8, 128, D)),
        "w_ffn_gate_c": chunk_cols(np.asarray(inp["w_ffn_gate"][0], dtype=f), [c_ * 128 for c_ in range(NFF)]),
        "w_ffn_up_c": chunk_cols(np.asarray(inp["w_ffn_up"][0], dtype=f), [c_ * 128 for c_ in range(NFF)]),
        "w_ffn_down": np.ascontiguousarray(np.asarray(inp["w_ffn_down"][0], dtype=f).reshape(NFF, 128, D)),
        "vecs": vecs,
        "ident": np.eye(128, dtype=f),
    }


def kernel(**inputs):
    inp = {k: np.asarray(v) for k, v in inputs.items()}
    nc, _ = build_nc()
    in_maps = [host_inputs(b, inp) for b in range(NCORES)]
    res = run_bass_kernel_spmd(nc, in_maps, core_ids=list(range(NCORES)))
    out = np.stack([np.asarray(res.results[b]["out"], dtype=np.float32) for b in range(NCORES)], axis=0)
    return out
```
